# Optimizing a Trainium2 kernel written in Bass

```python
import math
import jax, jax.numpy as jnp
from jax import lax
import numpy as np

D_MODEL = 1024
BATCH = 8
SEQ = 4096
DEPTH = 4

CHUNK = 64
N_A_LAYERS = DEPTH // 2
N_B_LAYERS = DEPTH - N_A_LAYERS
D_FF = 2816
EPS = 1e-6

SSM_EXPAND = 2
D_INNER = SSM_EXPAND * D_MODEL
SSM_HEAD_DIM = 64
SSM_HEADS = D_INNER // SSM_HEAD_DIM
SSM_GROUPS = 8
SSM_HEADS_PER_GROUP = SSM_HEADS // SSM_GROUPS
D_STATE = 128
CONV_W = 4
CONV_DIM = D_INNER + 2 * SSM_GROUPS * D_STATE
D_IN_PROJ = 2 * D_INNER + 2 * SSM_GROUPS * D_STATE + SSM_HEADS
SSD_CHUNK = CHUNK
DT_MIN = 1e-3
DT_MAX = 1e-1

ATT_HEADS = 16
ATT_HEAD_DIM = 64
ATT_DIM = ATT_HEADS * ATT_HEAD_DIM
LEFT_CHUNKS = 8
BAND = (LEFT_CHUNKS + 1) * CHUNK
PAD_LEN = LEFT_CHUNKS * CHUNK
MAX_REL = 128
N_REL = 2 * MAX_REL + 1

kernel_name = "hybrid_ssd_shared_kv_chunk_attn_macaron"


def rms_norm(x, g):
    xf = x.astype(jnp.float32)
    y = xf * lax.rsqrt(jnp.mean(xf * xf, axis=-1, keepdims=True) + EPS)
    return (y * g.astype(jnp.float32)).astype(x.dtype)


def swiglu(h, w_gate, w_up, w_down):
    return (jax.nn.silu(h @ w_gate) * (h @ w_up)) @ w_down


def causal_depthwise_conv(x, w, b):
    c = x.shape[-1]
    y = lax.conv_general_dilated(
        x, w.astype(x.dtype)[:, None, :], window_strides=(1,),
        padding=[(CONV_W - 1, 0)], dimension_numbers=("NWC", "WIO", "NWC"),
        feature_group_count=c)
    return y + b.astype(x.dtype)


def ssd_scan(X, A, Bm, Cm):
    b, S = X.shape[0], X.shape[1]
    nc, L = S // SSD_CHUNK, SSD_CHUNK
    G, R = SSM_GROUPS, SSM_HEADS_PER_GROUP
    X = X.reshape(b, nc, L, G, R, SSM_HEAD_DIM)
    A = A.reshape(b, nc, L, G, R).transpose(0, 3, 4, 1, 2)
    Bm = Bm.reshape(b, nc, L, G, D_STATE)
    Cm = Cm.reshape(b, nc, L, G, D_STATE)
    A_cs = jnp.cumsum(A, axis=-1)
    tril = jnp.tril(jnp.ones((L, L), dtype=bool))
    seg = A_cs[..., :, None] - A_cs[..., None, :]
    Lmat = jnp.exp(jnp.where(tril, seg, -jnp.inf))
    CB = jnp.einsum('bclgn,bcsgn->bgcls', Cm, Bm)
    W = CB[:, :, None] * Lmat
    Y_diag = jnp.einsum('bgrcls,bcsgrp->bclgrp', W, X)
    decay_states = jnp.exp(A_cs[..., -1:] - A_cs).transpose(0, 3, 4, 1, 2)
    states = jnp.einsum('bclgn,bclgrp->bcgrpn', Bm, X * decay_states[..., None])
    chunk_decay = jnp.exp(A_cs[..., -1])

    def step(h, inp):
        s_c, d_c = inp
        return h * d_c[..., None, None] + s_c, h

    h0 = jnp.zeros((b, G, R, SSM_HEAD_DIM, D_STATE), jnp.float32)
    _, prev = lax.scan(step, h0, (states.transpose(1, 0, 2, 3, 4, 5),
                                  chunk_decay.transpose(3, 0, 1, 2)))
    prev = prev.transpose(1, 0, 2, 3, 4, 5)
    out_decay = jnp.exp(A_cs).transpose(0, 3, 4, 1, 2)
    Y_off = jnp.einsum('bclgn,bcgrpn->bclgrp', Cm, prev) * out_decay[..., None]
    return (Y_diag + Y_off).reshape(b, S, SSM_HEADS, SSM_HEAD_DIM)


def mamba2_mixer(h, in_proj, conv_w, conv_b, dt_bias, A_log, D_skip, out_norm, out_proj):
    b, S, _ = h.shape
    zxbcdt = h @ in_proj
    z = zxbcdt[..., :D_INNER]
    xBC = zxbcdt[..., D_INNER:D_INNER + CONV_DIM]
    dt = zxbcdt[..., D_INNER + CONV_DIM:]
    xBC = jax.nn.silu(causal_depthwise_conv(xBC, conv_w, conv_b)).astype(jnp.float32)
    xs = xBC[..., :D_INNER].reshape(b, S, SSM_HEADS, SSM_HEAD_DIM)
    Bm = xBC[..., D_INNER:D_INNER + SSM_GROUPS * D_STATE].reshape(b, S, SSM_GROUPS, D_STATE)
    Cm = xBC[..., D_INNER + SSM_GROUPS * D_STATE:].reshape(b, S, SSM_GROUPS, D_STATE)
    dt = jax.nn.softplus(dt.astype(jnp.float32) + dt_bias.astype(jnp.float32))
    A = -jnp.exp(A_log.astype(jnp.float32))
    y = ssd_scan(xs * dt[..., None], dt * A, Bm, Cm)
    y = y + D_skip.astype(jnp.float32)[:, None] * xs
    gated = y.reshape(b, S, D_INNER) * jax.nn.silu(z.astype(jnp.float32))
    gg = gated.reshape(b, S, SSM_GROUPS, D_INNER // SSM_GROUPS)
    gg = gg * lax.rsqrt(jnp.mean(gg * gg, axis=-1, keepdims=True) + EPS)
    y = (gg.reshape(b, S, D_INNER) * out_norm.astype(jnp.float32)).astype(h.dtype)
    return y @ out_proj


def shared_kv(x, kv_norm, w_kv, k_norm):
    b, S, _ = x.shape
    kv = rms_norm(x, kv_norm) @ w_kv
    k = rms_norm(kv[..., :ATT_DIM].reshape(b, S, ATT_HEADS, ATT_HEAD_DIM), k_norm)
    v = kv[..., ATT_DIM:].reshape(b, S, ATT_HEADS, ATT_HEAD_DIM)
    pad = ((0, 0), (PAD_LEN, 0), (0, 0), (0, 0))
    return jnp.pad(k, pad), jnp.pad(v, pad)


def chunk_attention(h, k_pad, v_pad, w_q, q_norm, rel_bias, w_o):
    b, S, _ = h.shape
    nc = S // CHUNK
    q = rms_norm((h @ w_q).reshape(b, S, ATT_HEADS, ATT_HEAD_DIM), q_norm)
    q_chunks = q.reshape(b, nc, CHUNK, ATT_HEADS, ATT_HEAD_DIM).transpose(1, 0, 2, 3, 4)
    rel = (np.arange(BAND) - PAD_LEN)[None, :] - np.arange(CHUNK)[:, None]
    rel_idx = np.clip(rel, -MAX_REL, MAX_REL) + MAX_REL
    bias = rel_bias.astype(jnp.float32)[:, rel_idx]
    scale = 1.0 / math.sqrt(ATT_HEAD_DIM)
    band_off = jnp.arange(BAND, dtype=jnp.int32) - PAD_LEN

    def attend(args):
        q_c, c = args
        start = c * CHUNK
        k_b = lax.dynamic_slice_in_dim(k_pad, start, BAND, axis=1)
        v_b = lax.dynamic_slice_in_dim(v_pad, start, BAND, axis=1)
        s = jnp.einsum('bqhd,bkhd->bhqk', q_c, k_b,
                       preferred_element_type=jnp.float32) * scale + bias
        valid = (start + band_off) >= 0
        s = jnp.where(valid[None, None, None, :], s, -jnp.inf)
        p = jax.nn.softmax(s, axis=-1).astype(v_b.dtype)
        return jnp.einsum('bhqk,bkhd->bqhd', p, v_b)

    o = lax.map(attend, (q_chunks, jnp.arange(nc, dtype=jnp.int32)))
    o = o.transpose(1, 0, 2, 3, 4).reshape(b, S, ATT_DIM)
    return o @ w_o


def setup_inputs(seed: int = 0) -> dict:
    key = jax.random.key(seed)
    ks = jax.random.split(key, 32)
    f32 = jnp.float32

    def nrm(k, shape, scale):
        return jax.random.normal(k, shape, f32) * scale

    def gain(k, shape):
        return 1.0 + 0.02 * jax.random.normal(k, shape, f32)

    NA, NB = N_A_LAYERS, N_B_LAYERS
    u = jax.random.uniform(ks[14], (NA, SSM_HEADS), f32)
    dt0 = jnp.exp(u * (math.log(DT_MAX) - math.log(DT_MIN)) + math.log(DT_MIN))
    dt_bias = dt0 + jnp.log(-jnp.expm1(-dt0))
    return {
        "x": jax.random.normal(ks[0], (BATCH, SEQ, D_MODEL), f32),
        "ffn1_norm": gain(ks[1], (DEPTH, D_MODEL)),
        "ffn1_w_gate": nrm(ks[2], (DEPTH, D_MODEL, D_FF), D_MODEL ** -0.5),
        "ffn1_w_up": nrm(ks[3], (DEPTH, D_MODEL, D_FF), D_MODEL ** -0.5),
        "ffn1_w_down": nrm(ks[4], (DEPTH, D_FF, D_MODEL), D_FF ** -0.5),
        "ffn2_norm": gain(ks[5], (DEPTH, D_MODEL)),
        "ffn2_w_gate": nrm(ks[6], (DEPTH, D_MODEL, D_FF), D_MODEL ** -0.5),
        "ffn2_w_up": nrm(ks[7], (DEPTH, D_MODEL, D_FF), D_MODEL ** -0.5),
        "ffn2_w_down": nrm(ks[8], (DEPTH, D_FF, D_MODEL), D_FF ** -0.5),
        "ssm_norm": gain(ks[9], (NA, D_MODEL)),
        "ssm_in_proj": nrm(ks[10], (NA, D_MODEL, D_IN_PROJ), D_MODEL ** -0.5),
        "ssm_conv_w": nrm(ks[11], (NA, CONV_W, CONV_DIM), CONV_W ** -0.5),
        "ssm_conv_b": nrm(ks[12], (NA, CONV_DIM), 0.02),
        "ssm_dt_bias": dt_bias,
        "ssm_A_log": jnp.log(jax.random.uniform(ks[13], (NA, SSM_HEADS), f32, 1.0, 16.0)),
        "ssm_D": gain(ks[15], (NA, SSM_HEADS)),
        "ssm_out_norm": gain(ks[16], (NA, D_INNER)),
        "ssm_out_proj": nrm(ks[17], (NA, D_INNER, D_MODEL), D_INNER ** -0.5),
        "kv_norm": gain(ks[18], (D_MODEL,)),
        "w_kv": nrm(ks[19], (D_MODEL, 2 * ATT_DIM), D_MODEL ** -0.5),
        "k_norm": gain(ks[20], (ATT_HEAD_DIM,)),
        "att_norm": gain(ks[21], (NB, D_MODEL)),
        "att_w_q": nrm(ks[22], (NB, D_MODEL, ATT_DIM), D_MODEL ** -0.5),
        "att_q_norm": gain(ks[23], (NB, ATT_HEAD_DIM)),
        "att_rel_bias": nrm(ks[24], (NB, ATT_HEADS, N_REL), 0.1),
        "att_w_o": nrm(ks[25], (NB, ATT_DIM, D_MODEL), ATT_DIM ** -0.5),
    }


def reference(x, ffn1_norm, ffn1_w_gate, ffn1_w_up, ffn1_w_down,
              ffn2_norm, ffn2_w_gate, ffn2_w_up, ffn2_w_down,
              ssm_norm, ssm_in_proj, ssm_conv_w, ssm_conv_b, ssm_dt_bias,
              ssm_A_log, ssm_D, ssm_out_norm, ssm_out_proj,
              kv_norm, w_kv, k_norm,
              att_norm, att_w_q, att_q_norm, att_rel_bias, att_w_o):
    k_pad, v_pad = None, None
    for l in range(DEPTH):
        x = x + 0.5 * swiglu(rms_norm(x, ffn1_norm[l]), ffn1_w_gate[l], ffn1_w_up[l], ffn1_w_down[l])
        if l < N_A_LAYERS:
            x = x + mamba2_mixer(rms_norm(x, ssm_norm[l]), ssm_in_proj[l], ssm_conv_w[l],
                                 ssm_conv_b[l], ssm_dt_bias[l], ssm_A_log[l], ssm_D[l],
                                 ssm_out_norm[l], ssm_out_proj[l])
        else:
            if l == N_A_LAYERS:
                k_pad, v_pad = shared_kv(x, kv_norm, w_kv, k_norm)
            j = l - N_A_LAYERS
            x = x + chunk_attention(rms_norm(x, att_norm[j]), k_pad, v_pad, att_w_q[j],
                                    att_q_norm[j], att_rel_bias[j], att_w_o[j])
        x = x + 0.5 * swiglu(rms_norm(x, ffn2_norm[l]), ffn2_w_gate[l], ffn2_w_up[l], ffn2_w_down[l])
    return x
```

```python
import contextlib
import numpy as np
import concourse.bass as bass
import concourse.mybir as mybir
from concourse.bass_utils import run_bass_kernel_spmd

F32 = mybir.dt.float32
BF16 = mybir.dt.bfloat16
AF = mybir.ActivationFunctionType
ALU = mybir.AluOpType

D = 1024
KD = D // 128
DFF = 2816
KF = DFF // 128
EPS = 1e-6
NCORES = 8


class Sem:
    def __init__(self, h, dma):
        self.h = h
        self.dma = dma
        self.issued = 0


class Res:
    __slots__ = ("name", "w", "r", "dsem")

    def __init__(self, name, dsem=None):
        self.name = name
        self.w = None
        self.r = {}
        self.dsem = dsem


class Eng:
    def __init__(self, name, h, sem):
        self.name = name
        self.h = h
        self.sem = sem
        self.waited = {}


class Prog:
    def __init__(self, nc, es):
        self.nc = nc
        self.es = es
        self.nsem = 0
        self.pe = Eng("pe", nc.tensor, self.new_sem(False))
        self.act = Eng("act", nc.scalar, self.new_sem(False))
        self.dve = Eng("dve", nc.vector, self.new_sem(False))
        self.pool = Eng("pool", nc.gpsimd, self.new_sem(False))
        self.sp = Eng("sp", nc.sync, self.new_sem(False))
        self.engs = [self.pe, self.act, self.dve, self.pool, self.sp]

    def new_sem(self, dma):
        self.nsem += 1
        if not hasattr(self, "sems"):
            self.sems = []
        sm = Sem(self.es.enter_context(self.nc.semaphore("s%d" % self.nsem)), dma)
        self.sems.append(sm)
        return sm
        return Sem(self.es.enter_context(self.nc.semaphore("s%d" % self.nsem)), dma)

    def res(self, name, dma=False, share=None):
        if share is not None:
            return Res(name, share)
        if dma and getattr(self, "in_phase", False):
            if self.pool_idx == len(self.pool_list):
                self.pool_list.append(self.new_sem(True))
            sm = self.pool_list[self.pool_idx]
            self.pool_idx += 1
            return Res(name, sm)
        return Res(name, self.new_sem(True) if dma else None)

    def phase(self, on):
        if not hasattr(self, "pool_list"):
            self.pool_list = []
        self.pool_idx = 0
        self.in_phase = on

    def sb(self, name, shape, dt, es=None):
        self.nname = getattr(self, "nname", 0) + 1
        return (es or self.es).enter_context(
            self.nc.sbuf_tensor("sb_%s_%d" % (name, self.nname), list(shape), dt))

    def barrier(self):
        for e in self.engs:
            for sm in self.sems:
                if sm is e.sem or not sm.issued:
                    continue
                if e.waited.get(sm, 0) < sm.issued:
                    e.h.wait_ge(sm.h, sm.issued)
                    e.waited[sm] = sm.issued

    def ps(self, name, shape, dt=F32):
        return self.es.enter_context(self.nc.psum_tensor("ps_" + name, list(shape), dt))

    def _need(self, eng, stamp):
        if stamp is None:
            return
        sem, val = stamp
        if sem.dma:
            val = sem.issued
        elif sem is eng.sem:
            if eng.name == "pe" or val > sem.issued:
                return
        if eng.waited.get(sem, 0) >= val:
            return
        eng.h.wait_ge(sem.h, val)
        eng.waited[sem] = val

    def _deps(self, eng, reads, writes):
        for r in reads:
            self._need(eng, r.w)
        for w in writes:
            self._need(eng, w.w)
            for s, v in w.r.items():
                self._need(eng, (s, v))

    def op(self, eng, fn, reads=(), writes=(), sig=True):
        self._deps(eng, reads, writes)
        ins = fn(eng.h)
        if sig:
            eng.sem.issued += 1
            ins.then_inc(eng.sem.h, 1)
            stamp = (eng.sem, eng.sem.issued)
        else:
            stamp = (eng.sem, eng.sem.issued + 1)
        for r in reads:
            r.r[stamp[0]] = stamp[1]
        for w in writes:
            w.w = stamp
            w.r = {}
        return ins

    def dma(self, q, out, in_, reads, writes, **kw):
        self._deps(q, reads, writes)
        wres = writes[0]
        sem = wres.dsem
        ins = q.h.dma_start(out=out, in_=in_, **kw)
        sem.issued += 16
        ins.then_inc(sem.h, 16)
        stamp = (sem, sem.issued)
        for r in reads:
            r.r[sem] = sem.issued
        for w in writes:
            w.w = stamp
            w.r = {}
        return ins

    def finish(self, eng):
        for sem in self.sems:
            if sem.dma and sem.issued:
                self._need(eng, (sem, sem.issued))


DI = 2048
NH = 32
DS = 128
DINP = 6176
NCOLC = 960


class Cfg:
    def __init__(self, S, T=512, TM=256, n_ssm=2, n_att=2, mode="full"):
        self.S = S
        self.T = T
        self.TM = TM
        self.NT = S // T
        self.n_ssm = n_ssm
        self.n_att = n_att
        self.depth = n_ssm + n_att
        self.mode = mode


def make_consts():
    c = np.zeros((128, NCOLC), np.float32)
    c[:, 0:128] = np.eye(128)
    c[:, 128:256] = np.eye(128)[::-1]
    a = np.arange(64)
    c[0:64, 256:320] = (a[:, None] <= a[None, :])
    c[0:64, 320:384] = (a[:, None] > a[None, :])
    c[0:64, 384:448] = np.where(a[None, :] < a[:, None], -30000.0, 30000.0)
    c[:, 448:512] = 1.0
    c[:, 576 + 64:704] = 1.0
    p = np.arange(128)
    c[:, 704:832] = ((p[:, None] // 64) == (p[None, :] // 64))
    c[:, 832:960] = 1.0
    return {"cst": c}


def declare_inputs(nc, cfg):
    S = cfg.S
    L, NA, NB = cfg.depth, cfg.n_ssm, cfg.n_att
    t = {}

    def inp(name, shape, dt=F32):
        t[name] = nc.dram_tensor(name, list(shape), dt, kind="ExternalInput").ap()

    inp("x", [S, D])
    for f in ("ffn1", "ffn2"):
        inp(f + "_norm", [L, D])
        inp(f + "_w_gate", [L, D, DFF])
        inp(f + "_w_up", [L, D, DFF])
        inp(f + "_w_down", [L, DFF, D])
    if NA:
        inp("ssm_norm", [NA, D])
        inp("ssm_in_proj", [NA, D, DINP])
        inp("ssm_conv_w", [NA, 4, 4096])
        inp("ssm_conv_b", [NA, 4096])
        inp("ssm_dt_bias", [NA, NH])
        inp("ssm_A_log", [NA, NH])
        inp("ssm_D", [NA, NH])
        inp("ssm_out_norm", [NA, DI])
        inp("ssm_out_proj", [NA, DI, D])
    if NB:
        inp("kv_norm", [D])
        inp("w_kv", [D, 2 * D])
        inp("k_norm", [64])
        inp("att_norm", [NB, D])
        inp("att_w_q", [NB, D, D])
        inp("att_q_norm", [NB, 64])
        inp("att_rel_bias", [NB, 16, 257])
        inp("att_w_o", [NB, D, D])
    inp("cst", [128, NCOLC])
    t["out"] = nc.dram_tensor("out", [S, D], F32, kind="ExternalOutput").ap()
    return t


def mk(base, dims):
    return bass.AP(base.tensor, base.offset, [list(base.ap[0])] + [list(d) for d in dims])


def build(cfg):
    nc = bass.Bass("TRN2", target_bir_lowering=False)
    io = declare_inputs(nc, cfg)
    S, T, NT, L = cfg.S, cfg.T, cfg.NT, cfg.depth
    NA, NB = cfg.n_ssm, cfg.n_att

    def dram(name, shape, dt):
        return nc.dram_tensor(name, list(shape), dt, kind="Internal").ap()

    xres = dram("xres", [D, S], F32)
    wb = {}
    for f in ("ffn1", "ffn2"):
        wb[f + "_w_gate"] = dram(f + "_wg_b", [L, D, DFF], BF16)
        wb[f + "_w_up"] = dram(f + "_wu_b", [L, D, DFF], BF16)
        wb[f + "_w_down"] = dram(f + "_wd_b", [L, DFF, D], BF16)
    if NA:
        wb["ssm_in_proj"] = dram("ssm_in_b", [NA, D, DINP], BF16)
        wb["ssm_out_proj"] = dram("ssm_out_b", [NA, DI, D], BF16)
    if NB:
        wb["w_kv"] = dram("w_kv_b", [D, 2 * D], BF16)
        wb["att_w_q"] = dram("att_wq_b", [NB, D, D], BF16)
        wb["att_w_o"] = dram("att_wo_b", [NB, D, D], BF16)
        KTd = dram("KTd", [D, S], BF16)
        Vd = dram("Vd", [S, D], BF16)
        vext = dram("vext", [16, 384], F32)
    if NA:
        acsd = dram("acsd", [2, 128, 64], F32)

    with contextlib.ExitStack() as es:
        P = Prog(nc, es)
        pe, act, dve, pool, sp = P.pe, P.act, P.dve, P.pool, P.sp

        cst = P.sb("cst", [128, NCOLC], F32)
        cstb = P.sb("cstb", [128, NCOLC], BF16)
        ident = cst[:, 0:128]
        ones_b = cstb[:, 832:960]
        NG_ = 2 * L + NA + NB + 1
        gains = P.sb("gains", [128, NG_, KD], F32)
        eps_c = P.sb("eps_c", [128, 2], F32)
        r_ld = P.res("const_ld", dma=True)
        P.dma(sp, cst[:], io["cst"][:, :], [], [r_ld])
        gsrc = []
        for f in ("ffn1", "ffn2"):
            for l in range(L):
                gsrc.append(io[f + "_norm"][l, :])
        for l in range(NA):
            gsrc.append(io["ssm_norm"][l, :])
        for l in range(NB):
            gsrc.append(io["att_norm"][l, :])
        if NB:
            gsrc.append(io["kv_norm"][:])
        for i, g in enumerate(gsrc):
            P.dma(sp, gains[:, i, :], g.rearrange("(k p) -> p k", p=128), [], [r_ld],
                  allow_slow_non_contiguous=True)
        G_SSM, G_ATT, G_KV = 2 * L, 2 * L + NA, 2 * L + NA + NB
        r_const = P.res("const")
        P.op(dve, lambda e: e.tensor_copy(cstb[:], cst[:]), [r_ld], [r_const])
        P.op(dve, lambda e: e.memset(eps_c[:, 0:1], EPS), [], [r_const])
        P.op(dve, lambda e: e.memset(eps_c[:, 1:2], 1.0), [], [r_const])
        r_ones = r_const
        r_gains = r_const

        r_wconv3 = [[P.res("wconv%d_%d" % (l, i), dma=True) for i in range(3)] for l in range(L)]

        def conv_w(src, dst, r):
            rows = src.shape[0]
            for r0 in range(0, rows, 256):
                P.dma(pool, dst[r0:r0 + 256, :], src[r0:r0 + 256, :], [], [r])

        for l in range(L):
            for nm in ("_w_gate", "_w_up", "_w_down"):
                conv_w(io["ffn1" + nm][l], wb["ffn1" + nm][l], r_wconv3[l][0])
            if l < NA:
                conv_w(io["ssm_in_proj"][l], wb["ssm_in_proj"][l], r_wconv3[l][1])
                conv_w(io["ssm_out_proj"][l], wb["ssm_out_proj"][l], r_wconv3[l][1])
            else:
                j = l - NA
                if j == 0:
                    conv_w(io["w_kv"], wb["w_kv"], r_wconv3[l][1])
                conv_w(io["att_w_q"][j], wb["att_w_q"][j], r_wconv3[l][1])
                conv_w(io["att_w_o"][j], wb["att_w_o"][j], r_wconv3[l][1])
            for nm in ("_w_gate", "_w_up", "_w_down"):
                conv_w(io["ffn2" + nm][l], wb["ffn2" + nm][l], r_wconv3[l][2])

        r_xres = [P.res("xres%d" % i, dma=True) for i in range(NT)]
        r_out = P.res("out", dma=True)
        PS = P.ps("PS", [128, 4096])
        r_bank = [P.res("bank%d" % i) for i in range(8)]
        Q = [PS[:, i * 1024:(i + 1) * 1024] for i in range(4)]
        r_Q = [[r_bank[2 * i + h] for h in range(2)] for i in range(4)]
        pg = [Q[0][:, 0:512], Q[0][:, 512:1024]]
        pu = [Q[1][:, 0:512], Q[1][:, 512:1024]]
        py = [Q[2][:, 0:512], Q[2][:, 512:1024]]
        pst = Q[3][:, 0:512]
        ptr = Q[3][:, 512:1024]
        r_pg, r_pu, r_py = r_Q[0], r_Q[1], r_Q[2]
        r_pst, r_ptr = r_Q[3][0], r_Q[3][1]

        class Ph:
            pass

        def new_phase(Tt, nx=2):
            P.barrier()
            P.phase(True)
            ph = Ph()
            ph.es = contextlib.ExitStack()
            ph.T = Tt
            ph.xT = [P.sb("xT", [128, KD, Tt], F32, ph.es) for i in range(nx)]
            ph.r_xT = [P.res("xT%d" % i, dma=True) for i in range(nx)]
            ph.xn = P.sb("xn", [128, KD, Tt], BF16, ph.es)
            ph.r_xn = P.res("xn")
            ph.sq = P.sb("sq", [128, KD, Tt], BF16, ph.es)
            ph.r_sq = P.res("sq")
            ph.rstd = P.sb("rstd", [128, Tt], F32, ph.es)
            ph.r_rstd = P.res("rstd")
            return ph

        def end_phase(ph):
            P.barrier()
            P.phase(False)
            ph.es.close()

        def xres_tile(t0, Tt, rows=slice(0, D)):
            return xres[rows, t0:t0 + Tt].rearrange("(k p) t -> p k t", p=128)

        def prologue():
            ph = new_phase(T)
            NTM_ = 4
            tm = [P.sb("tm", [128, D], F32, ph.es) for i in range(NTM_)]
            r_tm = [P.res("tm%d" % i, dma=True) for i in range(NTM_)]
            nblk = S // 128
            for b in range(min(NTM_ - 1, nblk)):
                P.dma(sp, tm[b % NTM_][:], io["x"][b * 128:(b + 1) * 128, :], [], [r_tm[b % NTM_]])
            bk = 0
            for b in range(nblk):
                i = b % NTM_
                if b + NTM_ - 1 < nblk:
                    nb_ = b + NTM_ - 1
                    P.dma(sp, tm[nb_ % NTM_][:], io["x"][nb_ * 128:(nb_ + 1) * 128, :], [], [r_tm[nb_ % NTM_]])
                ti = b // (T // 128)
                xt = ph.xT[ti % 2]
                col = (b % (T // 128)) * 128
                for k4 in range(0, KD, 4):
                    bank = bk % 8
                    bk += 1
                    pb = PS[:, bank * 512:(bank + 1) * 512]
                    for k in range(k4, k4 + 4):
                        P.op(pe, lambda e: e.transpose(pb[:, (k - k4) * 128:(k - k4 + 1) * 128],
                                                       tm[i][:, k * 128:(k + 1) * 128], ident),
                             [r_tm[i], r_const], [r_bank[bank]], sig=(k == k4 + 3))
                    P.op(act, lambda e: e.activation(
                        out=xt[:, k4:k4 + 4, col:col + 128],
                        in_=pb.rearrange("p (k c) -> p k c", k=4), func=AF.Copy),
                        [r_bank[bank]], [ph.r_xT[ti % 2]])
                if (b + 1) % (T // 128) == 0:
                    P.dma(act, xres_tile(ti * T, T), xt[:], [ph.r_xT[ti % 2]], [r_xres[ti]])
            end_phase(ph)

        def epilogue():
            ph = new_phase(T)
            NTM_ = 4
            tm = [P.sb("tm", [128, D], F32, ph.es) for i in range(NTM_)]
            r_tm = [P.res("tm%d" % i, dma=True) for i in range(NTM_)]
            P.dma(sp, ph.xT[0][:], xres_tile(0, T), [r_xres[0]], [ph.r_xT[0]])
            bk = 0
            for ti in range(NT):
                xt = ph.xT[ti % 2]
                if ti + 1 < NT:
                    P.dma(sp, ph.xT[(ti + 1) % 2][:], xres_tile((ti + 1) * T, T), [r_xres[ti + 1]],
                          [ph.r_xT[(ti + 1) % 2]])
                for bb in range(T // 128):
                    b = ti * (T // 128) + bb
                    i = b % NTM_
                    for k4 in range(0, KD, 4):
                        bank = bk % 8
                        bk += 1
                        pb = PS[:, bank * 512:(bank + 1) * 512]
                        for k in range(k4, k4 + 4):
                            P.op(pe, lambda e: e.transpose(
                                pb[:, (k - k4) * 128:(k - k4 + 1) * 128],
                                xt[:, k, bb * 128:(bb + 1) * 128], ident),
                                [ph.r_xT[ti % 2], r_const], [r_bank[bank]], sig=(k == k4 + 3))
                        P.op(act, lambda e: e.activation(out=tm[i][:, k4 * 128:(k4 + 4) * 128],
                                                         in_=pb, func=AF.Copy),
                             [r_bank[bank]], [r_tm[i]])
                    P.dma(act, io["out"][b * 128:(b + 1) * 128, :], tm[i][:], [r_tm[i]], [r_out])
            end_phase(ph)

        def rms_sq(ph, slot):
            xt = ph.xT[slot]
            P.op(act, lambda e: e.activation(out=ph.sq[:], in_=xt[:], func=AF.Square),
                 [ph.r_xT[slot]], [ph.r_sq])

        def rms_stats(ph, slot, do_sq=True):
            xt = ph.xT[slot]
            Tt = ph.T
            if do_sq:
                rms_sq(ph, slot)
            for s0 in range(0, Tt, 512):
                w = min(512, Tt - s0)
                cs = slice(s0, s0 + w)
                for k in range(KD):
                    P.op(pe, lambda e: e.matmul(pst[:, 0:w], ones_b, ph.sq[:, k, cs],
                                                start=(k == 0), stop=(k == KD - 1)),
                         [r_ones, ph.r_sq], [r_pst], sig=(k == KD - 1))
                P.op(act, lambda e: e.activation(out=ph.rstd[:, cs], in_=pst[:, 0:w], func=AF.Ln,
                                                 bias=eps_c[:, 0:1], scale=1.0 / D),
                     [r_pst, r_const], [ph.r_rstd])
                P.op(act, lambda e: e.activation(out=ph.rstd[:, cs], in_=ph.rstd[:, cs], func=AF.Exp, scale=-0.5),
                     [ph.r_rstd], [ph.r_rstd])

        def rms_apply(ph, slot, gidx, xn_t, r_xn_t):
            xt = ph.xT[slot]
            for k in range(KD):
                P.op(dve, lambda e: e.scalar_tensor_tensor(
                    out=xn_t[:, k, :], in0=xt[:, k, :], scalar=gains[:, gidx, k:k + 1], in1=ph.rstd[:],
                    op0=ALU.mult, op1=ALU.mult),
                    [ph.r_xT[slot], r_gains, ph.r_rstd], [r_xn_t])

        def ffn(l, fidx):
            ph = new_phase(T)
            NS = T // 512
            f = ("ffn1", "ffn2")[fidx]
            Wg, Wu, Wd = wb[f + "_w_gate"][l], wb[f + "_w_up"][l], wb[f + "_w_down"][l]
            gidx = fidx * L + l
            GH, GO = 2, 2
            NG, NO = KF // GH, KD // GO
            NB3 = 3
            wg = [P.sb("wg", [128, KD, GH * 128], BF16, ph.es) for i in range(NB3)]
            wu = [P.sb("wu", [128, KD, GH * 128], BF16, ph.es) for i in range(NB3)]
            r_wg = [P.res("wg%d" % i, dma=True) for i in range(NB3)]
            r_wu = [P.res("wu%d" % i, dma=True) for i in range(NB3)]
            wd = [P.sb("wd", [128, KF, GO * 128], BF16, ph.es) for i in range(2)]
            r_wd = [P.res("wd%d" % i, dma=True) for i in range(2)]
            hact = P.sb("hact", [128, KF, T], BF16, ph.es)
            r_hact = [P.res("hact%d" % j) for j in range(KF)]
            sg = [P.sb("sg", [128, 512], F32, ph.es) for i in range(2)]
            r_sg = [P.res("sg%d" % i) for i in range(2)]
            xo = [P.sb("xo", [128, GO, T], F32, ph.es) for i in range(2)]
            r_xo = [P.res("xo%d" % i, dma=True) for i in range(2)]
            wctr = [0, 0]

            def load_gu(g):
                i = wctr[0] % NB3
                wctr[0] += 1
                cs = slice(g * GH * 128, (g + 1) * GH * 128)
                P.dma(sp, wg[i][:], Wg[:, cs].rearrange("(k p) c -> p k c", p=128), [r_wconv3[l][2 * fidx]], [r_wg[i]])
                P.dma(sp, wu[i][:], Wu[:, cs].rearrange("(k p) c -> p k c", p=128), [r_wconv3[l][2 * fidx]], [r_wu[i]])
                return i

            def load_d(o):
                i = wctr[1] % 2
                wctr[1] += 1
                cs = slice(o * GO * 128, (o + 1) * GO * 128)
                P.dma(sp, wd[i][:], Wd[:, cs].rearrange("(k p) c -> p k c", p=128), [r_wconv3[l][2 * fidx]], [r_wd[i]])
                return i

            pcnt = [0, 0]
            xn2 = [ph.xn, P.sb("xn2", [128, KD, T], BF16, ph.es)]
            r_xn2 = [ph.r_xn, P.res("xn2")]

            def norm_a(ti):
                P.dma(sp, ph.xT[ti % 2][:], xres_tile(ti * T, T), [r_xres[ti]], [ph.r_xT[ti % 2]])
                rms_sq(ph, ti % 2)

            def norm_b(ti):
                rms_stats(ph, ti % 2, do_sq=False)
                rms_apply(ph, ti % 2, gidx, xn2[ti % 2], r_xn2[ti % 2])

            norm_a(0)
            norm_b(0)
            pre_gu = load_gu(0)
            for ti in range(NT):
                slot = ti % 2
                xt = ph.xT[slot]
                xn = xn2[slot]
                r_xn_t = r_xn2[slot]
                gq_ = [pre_gu, load_gu(1)]
                pre_d = None
                for g in range(NG):
                    if g == NG - 3:
                        pre_d = load_d(0)
                    if ti + 1 < NT and g == 1:
                        norm_a(ti + 1)
                    if ti + 1 < NT and g == 6:
                        norm_b(ti + 1)
                    wi = gq_.pop(0)
                    if g + 2 < NG:
                        gq_.append(load_gu(g + 2))
                    for jj in range(GH):
                        j = g * GH + jj
                        for s in range(NS):
                            pi = pcnt[0] % 2
                            pcnt[0] += 1
                            cs = slice(s * 512, (s + 1) * 512)
                            for k in range(KD):
                                P.op(pe, lambda e: e.matmul(pg[pi], wg[wi][:, k, jj * 128:(jj + 1) * 128],
                                                            xn[:, k, cs], start=(k == 0), stop=(k == KD - 1)),
                                     [r_wg[wi], r_xn_t], [r_pg[pi]], sig=(k == KD - 1))
                            for k in range(KD):
                                P.op(pe, lambda e: e.matmul(pu[pi], wu[wi][:, k, jj * 128:(jj + 1) * 128],
                                                            xn[:, k, cs], start=(k == 0), stop=(k == KD - 1)),
                                     [r_wu[wi], r_xn_t], [r_pu[pi]], sig=(k == KD - 1))
                            P.op(act, lambda e: e.activation(out=sg[pi][:], in_=pg[pi], func=AF.Silu),
                                 [r_pg[pi]], [r_sg[pi]])
                            P.op(dve, lambda e: e.tensor_tensor(hact[:, j, cs], sg[pi][:], pu[pi], ALU.mult),
                                 [r_sg[pi], r_pu[pi]], [r_hact[j]])
                nxt = pre_d
                for o in range(NO):
                    wi = nxt
                    if o + 1 < NO:
                        nxt = load_d(o + 1)
                    if o == 2 and ti + 1 < NT:
                        pre_gu = load_gu(0)
                    oi = o % 2
                    for mm in range(GO):
                        m = o * GO + mm
                        for s in range(NS):
                            pi = pcnt[1] % 2
                            pcnt[1] += 1
                            cs = slice(s * 512, (s + 1) * 512)
                            for j in range(KF):
                                P.op(pe, lambda e: e.matmul(py[pi], wd[wi][:, j, mm * 128:(mm + 1) * 128],
                                                            hact[:, j, cs], start=(j == 0), stop=(j == KF - 1)),
                                     [r_wd[wi], r_hact[j]], [r_py[pi]], sig=(j == KF - 1))
                            P.op(dve, lambda e: e.scalar_tensor_tensor(
                                out=xo[oi][:, mm, cs], in0=py[pi], scalar=0.5, in1=xt[:, m, cs],
                                op0=ALU.mult, op1=ALU.add),
                                [r_py[pi], ph.r_xT[slot]], [r_xo[oi]])
                    P.dma(act, xres_tile(ti * T, T, slice(o * GO * 128, (o + 1) * GO * 128)), xo[oi][:],
                          [r_xo[oi], ph.r_xT[slot]], [r_xres[ti]])
            end_phase(ph)

        def mamba(l):
            TM = cfg.TM
            CH = TM // 64
            ph = new_phase(TM, nx=1)
            xt, xn = ph.xT[0], ph.xn
            Win = wb["ssm_in_proj"][l]
            Wout = wb["ssm_out_proj"][l]
            sbp = lambda n, sh, dt: P.sb(n, sh, dt, ph.es)
            convw = sbp("convw", [128, 32, 4], F32)
            convb = sbp("convb", [128, 32], F32)
            onorm = sbp("onorm", [128, 16], F32)
            Dcol = sbp("Dcol", [128, 16], F32)
            dtb = sbp("dtb", [64, 32], F32)
            Abc = sbp("Abc", [64, 32], F32)
            wdt = sbp("wdt", [128, KD, 32], BF16)
            halo = sbp("halo", [128, 32, 3], F32)
            st = sbp("st", [128, 32, 64], F32)
            stbp = sbp("stbp", [128, 32, 128], BF16)
            Xp = sbp("Xp", [64, 32, 128], BF16)
            Xd = [sbp("Xd", [64, 32, 64], BF16) for i in range(2)]
            cvx = sbp("cvx", [128, 16, TM], F32)
            zs = sbp("zs", [128, 16, TM], BF16)
            ynb = sbp("ynb", [128, 16, TM], BF16)
            BT = sbp("BT", [128, 8, TM], BF16)
            CT = sbp("CT", [128, 8, TM], BF16)
            raw = [sbp("raw", [128, TM + 3], F32) for i in range(2)]
            acc = [sbp("acc", [128, TM], F32) for i in range(2)]
            NWI = 3
            winp = [sbp("winp", [128, KD, 512], BF16) for i in range(NWI)]
            wo = [winp[i][:].rearrange("p k c -> p (k c)").rearrange("p (k c) -> p k c", k=16) for i in range(NWI)]
            xo = [sbp("xo", [128, 2, TM], F32) for i in range(2)]
            dtt = sbp("dtt", [64, CH * 32], F32)
            dte = sbp("dte", [64, CH * 32], F32)
            dtA = sbp("dtA", [64, CH * 32], F32)
            acs = sbp("acs", [64, CH * 32], F32)
            dec = sbp("dec", [64, CH * 32], F32)
            dtd = sbp("dtd", [64, CH * 32], F32)
            cdec = sbp("cdec", [128, CH * 32], F32)
            AcsRowB = [sbp("AcsRowB", [128, 32, 64], F32) for i in range(2)]
            dseg = sbp("dseg", [64, 32, 64], F32)
            WT = sbp("WT", [64, 32, 64], BF16)
            Btok = [sbp("Btok", [64, 8, 128], BF16) for i in range(2)]
            eA2 = [sbp("eA2", [128, 1024], F32) for i in range(2)]
            ydsb = [sbp("ydsb", [128, 1024], F32) for i in range(2)]
            tY = sbp("tY", [128, 1024], F32)
            ybuf = sbp("ybuf", [128, 16, TM], F32)
            acsT = sbp("acsT", [128, 64], F32)
            cbs = sbp("cbs", [64, 8, 64], F32)
            r_cbs = P.res("cbs")
            r_WT2 = P.res("WT2")
            r_dsegh = [P.res("dseg0"), P.res("dseg1")]
            R = lambda n, dma=False: P.res(n, dma=dma)
            r_par = R("par", True)
            r_halo, r_st, r_stbp, r_Xp, r_cvx, r_zs, r_ynb = [R(n) for n in (
                "halo", "st", "stbp", "Xp", "cvx", "zs", "ynb")]
            sq2, r_sq2 = zs, r_zs
            r_BT, r_CT = R("BT"), R("CT")
            r_raw = [R("raw0"), R("raw1")]
            r_acc = [R("acc0"), R("acc1")]
            r_winp = [R("winp%d" % i, True) for i in range(NWI)]
            r_wo = r_winp
            r_xo = [R("xo0"), R("xo1")]
            r_dt = R("dt")
            r_WT, r_tY, r_dseg, r_ybuf, r_acsT = [R(n) for n in ("WT", "tY", "dseg", "ybuf", "acsT")]
            rg, r_rg = tY, r_tY
            r_Btok = [R("Btok0"), R("Btok1")]
            r_eA2 = [R("eA20"), R("eA21")]
            r_ydsb = [R("ydsb0"), R("ydsb1")]
            r_Xd = [R("Xd0"), R("Xd1")]
            r_ARB = [R("ARB0", True), R("ARB1", True)]
            r_acsd = [R("acsd0", True), R("acsd1", True)]

            for t in range(4):
                for j0 in range(0, 32, 8):
                    P.dma(sp, convw[:, j0:j0 + 8, t],
                          io["ssm_conv_w"][l, t, j0 * 128:(j0 + 8) * 128].rearrange("(j p) -> p j", p=128), [], [r_par],
                          allow_slow_non_contiguous=True)
            for j0 in range(0, 32, 8):
                P.dma(sp, convb[:, j0:j0 + 8],
                      io["ssm_conv_b"][l, j0 * 128:(j0 + 8) * 128].rearrange("(j p) -> p j", p=128), [], [r_par],
                      allow_slow_non_contiguous=True)
            for j0 in range(0, 16, 8):
                P.dma(sp, onorm[:, j0:j0 + 8],
                      io["ssm_out_norm"][l, j0 * 128:(j0 + 8) * 128].rearrange("(j p) -> p j", p=128), [], [r_par],
                      allow_slow_non_contiguous=True)
            dsrc = io["ssm_D"][l, :]
            for hh in range(2):
                P.dma(sp, Dcol[hh * 64:(hh + 1) * 64, :],
                      bass.AP(dsrc.tensor, dsrc.offset + hh, [[0, 64], [2, 16]]), [], [r_par],
                      allow_slow_non_contiguous=True)
            for dst_, nm in ((dtb, "ssm_dt_bias"), (Abc, "ssm_A_log")):
                src = io[nm][l, :]
                P.dma(sp, dst_[:], bass.AP(src.tensor, src.offset, [[0, 64], [1, 32]]), [], [r_par])
            P.dma(sp, wdt[:], Win[:, 6144:6176].rearrange("(k p) c -> p k c", p=128), [r_wconv3[l][1]], [r_par])
            P.op(act, lambda e: e.activation(out=Abc[:], in_=Abc[:], func=AF.Exp), [r_par], [r_par])
            P.op(dve, lambda e: e.tensor_scalar(Abc[:], Abc[:], -1.0, None, ALU.mult), [r_par], [r_par])
            P.op(pool, lambda e: e.memset(halo[:], 0.0), [], [r_halo])
            P.op(pool, lambda e: e.memset(st[:], 0.0), [], [r_st])
            P.op(pool, lambda e: e.memset(stbp[:], 0.0), [], [r_stbp])
            P.op(pool, lambda e: e.memset(Xp[:], 0.0), [], [r_Xp])

            wctr = [0, 0]

            def load_in(g):
                i = wctr[0] % NWI
                wctr[0] += 1
                P.dma(sp, winp[i][:], Win[:, g * 512:(g + 1) * 512].rearrange("(k p) c -> p k c", p=128),
                      [r_wconv3[l][1]], [r_winp[i]])
                return i

            def load_o(o):
                i = wctr[0] % NWI
                wctr[0] += 1
                P.dma(sp, wo[i], Wout[:, o * 256:(o + 1) * 256].rearrange("(k p) c -> p k c", p=128),
                      [r_wconv3[l][1]], [r_wo[i]])
                return i

            def padded(t, hf):
                b = t[:, hf * 16, 0:1]
                return mk(b, [[256, 8], [192, 2], [1, 64]])

            pcnt = [0]
            pre_in = [None]
            for ti in range(S // TM):
                t0 = ti * TM
                rx = r_xres[t0 // T]
                P.dma(sp, xt[:], xres_tile(t0, TM), [rx], [ph.r_xT[0]])
                rms_stats(ph, 0)
                rms_apply(ph, 0, G_SSM + l, xn, ph.r_xn)
                N_ = CH * 32
                for c in range(CH):
                    for k in range(KD):
                        P.op(pe, lambda e: e.matmul(Q[3][0:64, c * 32:(c + 1) * 32], xn[:, k, c * 64:(c + 1) * 64],
                                                    wdt[:, k, :], start=(k == 0), stop=(k == KD - 1)),
                             [ph.r_xn, r_par], [r_Q[3][0]], sig=(k == KD - 1))
                P.op(dve, lambda e: e.tensor_tensor(mk(dtt[:, 0:1], [[32, CH], [1, 32]]),
                                                    mk(Q[3][0:64, 0:1], [[32, CH], [1, 32]]),
                                                    mk(dtb[:, 0:1], [[0, CH], [1, 32]]), ALU.add),
                     [r_Q[3][0], r_par], [r_dt])
                P.op(act, lambda e: e.activation(out=dte[:], in_=dtt[:], func=AF.Exp), [r_dt], [r_dt])
                P.op(act, lambda e: e.activation(out=dtt[:], in_=dte[:], func=AF.Ln, bias=eps_c[0:64, 1:2], scale=1.0),
                     [r_dt, r_const], [r_dt])
                P.op(dve, lambda e: e.tensor_tensor(mk(dtA[:, 0:1], [[32, CH], [1, 32]]),
                                                    mk(dtt[:, 0:1], [[32, CH], [1, 32]]),
                                                    mk(Abc[:, 0:1], [[0, CH], [1, 32]]), ALU.mult),
                     [r_dt, r_par], [r_dt])
                P.op(pe, lambda e: e.matmul(Q[3][0:64, 128:128 + N_], cst[0:64, 256:320], dtA[:], start=True, stop=True),
                     [r_dt, r_const], [r_Q[3][0]], sig=False)
                P.op(pe, lambda e: e.matmul(Q[3][0:64, 256:256 + N_], cst[0:64, 320:384], dtA[:], start=True, stop=True),
                     [r_dt, r_const], [r_Q[3][0]], sig=False)
                P.op(pe, lambda e: e.matmul(Q[3][:, 384:384 + N_], cst[0:64, 832:960], dtA[:], start=True, stop=True),
                     [r_dt, r_const], [r_Q[3][0]])
                P.op(act, lambda e: e.activation(out=acs[:], in_=Q[3][0:64, 128:128 + N_], func=AF.Copy),
                     [r_Q[3][0]], [r_dt])
                P.op(act, lambda e: e.activation(out=dec[:], in_=Q[3][0:64, 256:256 + N_], func=AF.Exp),
                     [r_Q[3][0]], [r_dt])
                P.op(act, lambda e: e.activation(out=cdec[:], in_=Q[3][:, 384:384 + N_], func=AF.Exp),
                     [r_Q[3][0]], [r_dt])
                P.op(dve, lambda e: e.tensor_tensor(dtd[:], dtt[:], dec[:], ALU.mult), [r_dt], [r_dt])
                wq_ = [pre_in[0] if pre_in[0] is not None else load_in(0)]
                pre_in[0] = None
                for g in range(12):
                    wi = wq_.pop(0)
                    if g + 1 < 12:
                        wq_.append(load_in(g + 1))
                    for jp in range(2):
                        pair = []
                        for jj in (2 * jp, 2 * jp + 1):
                            j = g * 4 + jj
                            pi = pcnt[0] % 2
                            pcnt[0] += 1
                            for k in range(KD):
                                P.op(pe, lambda e: e.matmul(pg[pi][:, 0:TM], winp[wi][:, k, jj * 128:(jj + 1) * 128],
                                                            xn[:, k, :], start=(k == 0), stop=(k == KD - 1)),
                                     [r_winp[wi], ph.r_xn], [r_pg[pi]], sig=(k == KD - 1))
                            pair.append((j, pi))
                        if g < 4:
                            for j, pi in pair:
                                P.op(act, lambda e: e.activation(out=zs[:, j, :], in_=pg[pi][:, 0:TM], func=AF.Silu),
                                     [r_pg[pi]], [r_zs])
                            continue
                        ch = [(j - 16, pi, (j - 16) % 2) for j, pi in pair]
                        for jc, pi, ri in ch:
                            P.op(act, lambda e: e.activation(out=raw[ri][:, 3:3 + TM], in_=pg[pi][:, 0:TM], func=AF.Copy),
                                 [r_pg[pi]], [r_raw[ri]])
                            P.op(pool, lambda e: e.tensor_copy(raw[ri][:, 0:3], halo[:, jc, :]), [r_halo], [r_raw[ri]])
                        for jc, pi, ri in ch:
                            P.op(act, lambda e: e.activation(out=acc[ri][:], in_=pg[pi][:, 0:TM], func=AF.Identity,
                                                             scale=convw[:, jc, 3:4], bias=convb[:, jc:jc + 1]),
                                 [r_pg[pi], r_par], [r_acc[ri]])
                        for t in (2, 1, 0):
                            for jc, pi, ri in ch:
                                P.op(dve, lambda e: e.scalar_tensor_tensor(out=acc[ri][:], in0=raw[ri][:, t:t + TM],
                                                                           scalar=convw[:, jc, t:t + 1], in1=acc[ri][:],
                                                                           op0=ALU.mult, op1=ALU.add),
                                     [r_raw[ri], r_par, r_acc[ri]], [r_acc[ri]])
                        for jc, pi, ri in ch:
                            P.op(pool, lambda e: e.tensor_copy(halo[:, jc, :], raw[ri][:, TM:TM + 3]), [r_raw[ri]], [r_halo])
                            if jc < 16:
                                dst_, rd = cvx[:, jc, :], r_cvx
                            elif jc < 24:
                                dst_, rd = BT[:, jc - 16, :], r_BT
                            else:
                                dst_, rd = CT[:, jc - 24, :], r_CT
                            P.op(act, lambda e: e.activation(out=dst_, in_=acc[ri][:], func=AF.Silu), [r_acc[ri]], [rd])
                tp = ti % 2
                P.op(pe, lambda e: e.transpose(Q[3][:, 0:64], acs[0:64, 0:N_], cst[0:64, 0:64]),
                     [r_dt, r_const], [r_Q[3][0]])
                P.op(act, lambda e: e.activation(out=acsT[0:N_, :], in_=Q[3][0:N_, 0:64], func=AF.Copy),
                     [r_Q[3][0]], [r_acsT])
                P.dma(sp, acsd[tp, 0:N_, :], acsT[0:N_, :], [r_acsT], [r_acsd[tp]])

                def load_arb(c):
                    b = c % 2
                    src = acsd[tp, c * 32:(c + 1) * 32, :]
                    P.dma(sp, AcsRowB[b][:].rearrange("p h l -> p (h l)"),
                          bass.AP(src.tensor, src.offset, [[0, 128], [1, 2048]]), [r_acsd[tp]], [r_ARB[b]])

                def front(c):
                    b = c % 2
                    cs = slice(c * 64, (c + 1) * 64)
                    a_c = acs[:, c * 32:c * 32 + 1]
                    if c + 1 < CH:
                        load_arb(c + 1)
                    for g in range(8):
                        P.op(pe, lambda e: e.matmul(Q[2][0:64, g * 128:(g + 1) * 128], BT[:, g, cs], cstb[:, 0:128],
                                                    start=True, stop=True),
                             [r_BT, r_const], [r_Q[2][0], r_Q[2][1]], sig=(g == 7))
                    P.op(act, lambda e: e.activation(out=Btok[b][:].rearrange("p g n -> p (g n)"), in_=Q[2][0:64, :],
                                                     func=AF.Copy), [r_Q[2][0], r_Q[2][1]], [r_Btok[b]])
                    for g in range(8):
                        P.op(pe, lambda e: e.matmul(Q[3][0:64, 512 + g * 64:512 + (g + 1) * 64], BT[:, g, cs],
                                                    CT[:, g, cs], start=True, stop=True),
                             [r_BT, r_CT], [r_Q[3][1]], sig=(g == 7))
                    P.op(act, lambda e: e.activation(out=cbs[:].rearrange("p g l -> p (g l)"), in_=Q[3][0:64, 512:1024],
                                                     func=AF.Copy), [r_Q[3][1]], [r_cbs])
                    for hf in range(2):
                        hs = slice(hf * 16, (hf + 1) * 16)
                        P.op(pool, lambda e: e.tensor_tensor(dseg[:, hs, :], AcsRowB[b][0:64, hs, :],
                                                             mk(acs[:, c * 32 + hf * 16:c * 32 + hf * 16 + 1], [[1, 16], [0, 64]]),
                                                             ALU.subtract), [r_ARB[b], r_dt], [r_dsegh[hf]])
                    for hf in range(2):
                        hs = slice(hf * 16, (hf + 1) * 16)
                        P.op(dve, lambda e: e.tensor_tensor(dseg[:, hs, :], dseg[:, hs, :],
                                                            mk(cst[0:64, 384:385], [[0, 16], [1, 64]]), ALU.min),
                             [r_dsegh[hf], r_const], [r_dsegh[hf]])
                        P.op(act, lambda e: e.activation(out=dseg[:, hs, :], in_=dseg[:, hs, :], func=AF.Exp),
                             [r_dsegh[hf]], [r_dsegh[hf]])
                    P.op(dve, lambda e: e.tensor_tensor(
                        mk(WT[:, 0, 0:1], [[256, 4], [64, 4], [1, 64]]),
                        mk(dseg[:, 0, 0:1], [[256, 4], [64, 4], [1, 64]]),
                        mk(cbs[:, 0, 0:1], [[64, 4], [0, 4], [1, 64]]), ALU.mult),
                        [r_dsegh[0], r_cbs], [r_WT])
                    for g in range(4, 8):
                        P.op(pool, lambda e: e.tensor_tensor(WT[:, 4 * g:4 * g + 4, :], dseg[:, 4 * g:4 * g + 4, :],
                                                             mk(cbs[:, g, 0:1], [[0, 4], [1, 64]]), ALU.mult),
                             [r_dsegh[1], r_cbs], [r_WT2])
                    for hf in range(2):
                        for i in range(8):
                            P.op(pe, lambda e: e.transpose(Q[0][0:64, i * 128:(i + 1) * 128], cvx[:, hf * 8 + i, cs],
                                                           ident), [r_cvx, r_const], [r_Q[0][0], r_Q[0][1]],
                                 sig=(i == 7))
                        hb = c * 32 + hf * 16
                        P.op(dve, lambda e: e.tensor_tensor(padded(Xp, hf), mk(Q[0][0:64, 0:1], [[128, 8], [64, 2], [1, 64]]),
                                                            mk(dtt[:, hb:hb + 1], [[2, 8], [1, 2], [0, 64]]), ALU.mult),
                             [r_Q[0][0], r_Q[0][1], r_dt], [r_Xp])
                        P.op(dve, lambda e: e.tensor_tensor(Xd[b][:, hf * 16:(hf + 1) * 16, :],
                                                            mk(Q[0][0:64, 0:1], [[64, 16], [1, 64]]),
                                                            mk(dtd[:, hb:hb + 1], [[1, 16], [0, 64]]), ALU.mult),
                             [r_Q[0][0], r_Q[0][1], r_dt], [r_Xd[b]])
                    for i in range(16):
                        for hh in range(2):
                            P.op(pe, lambda e: e.matmul(Q[2][:, i * 64:(i + 1) * 64], Xp[:, 2 * i + hh, :],
                                                        WT[:, 2 * i + hh, :], start=(hh == 0), stop=(hh == 1)),
                                 [r_Xp, r_WT, r_WT2], [r_Q[2][i // 8]], sig=(hh == 1 and i % 8 == 7))
                    P.op(act, lambda e: e.activation(out=ydsb[b][:], in_=Q[2][:, :], func=AF.Copy),
                         [r_Q[2][0], r_Q[2][1]], [r_ydsb[b]])
                    for hh in range(2):
                        ps_ = slice(hh * 64, (hh + 1) * 64)
                        P.op(act, lambda e: e.activation(out=eA2[b][ps_, :].rearrange("p (i l) -> p i l", l=64),
                                                         in_=mk(AcsRowB[b][ps_, hh, 0:1], [[128, 16], [1, 64]]),
                                                         func=AF.Exp), [r_ARB[b]], [r_eA2[b]])

                def back(c):
                    b = c % 2
                    cs = slice(c * 64, (c + 1) * 64)
                    for i in range(16):
                        for hh in range(2):
                            P.op(pe, lambda e: e.matmul(Q[1][:, i * 64:(i + 1) * 64], stbp[:, 2 * i + hh, :],
                                                        CT[:, i // 2, cs], start=(hh == 0), stop=(hh == 1)),
                                 [r_stbp, r_CT], [r_Q[1][i // 8]], sig=(hh == 1 and i % 8 == 7))
                    P.op(pool, lambda e: e.tensor_tensor(st[:], st[:], mk(cdec[:, c * 32:c * 32 + 1], [[1, 32], [0, 64]]),
                                                         ALU.mult), [r_st, r_dt], [r_st])
                    P.op(dve, lambda e: e.tensor_tensor(tY[:], Q[1][:, :], eA2[b][:], ALU.mult),
                         [r_Q[1][0], r_Q[1][1], r_eA2[b]], [r_tY])
                    P.op(pool, lambda e: e.tensor_tensor(ybuf[:, :, cs], mk(tY[:, 0:1], [[64, 16], [1, 64]]),
                                                         mk(ydsb[b][:, 0:1], [[64, 16], [1, 64]]), ALU.add),
                         [r_tY, r_ydsb[b]], [r_ybuf])
                    for hf in range(2):
                        for gi in range(4):
                            g = hf * 4 + gi
                            P.op(pe, lambda e: e.matmul(Q[1][:, gi * 256:(gi + 1) * 256], Btok[b][:, g, :],
                                                        Xd[b][:, g * 4:(g + 1) * 4, :], start=True, stop=True),
                                 [r_Btok[b], r_Xd[b]], [r_Q[1][gi // 2]], sig=(gi % 2 == 1))
                        sth = st[:, hf * 16:(hf + 1) * 16, :]
                        P.op(dve, lambda e: e.tensor_tensor(sth, sth, mk(Q[1][:, 0:1], [[64, 16], [1, 64]]), ALU.add),
                             [r_st, r_Q[1][0], r_Q[1][1]], [r_st])
                        P.op(act, lambda e: e.activation(out=padded(stbp, hf),
                                                         in_=mk(st[:, hf * 16, 0:1], [[128, 8], [64, 2], [1, 64]]),
                                                         func=AF.Copy), [r_st], [r_stbp])

                load_arb(0)
                front(0)
                for c in range(CH):
                    if c + 1 < CH:
                        front(c + 1)
                    back(c)
                for k in range(16):
                    P.op(dve, lambda e: e.scalar_tensor_tensor(out=cvx[:, k, :], in0=cvx[:, k, :], scalar=Dcol[:, k:k + 1],
                                                               in1=ybuf[:, k, :], op0=ALU.mult, op1=ALU.add),
                         [r_cvx, r_ybuf, r_par], [r_cvx])
                P.op(dve, lambda e: e.tensor_tensor(cvx[:], cvx[:], zs[:], ALU.mult), [r_cvx, r_zs], [r_cvx])
                P.op(act, lambda e: e.activation(out=sq2[:], in_=cvx[:], func=AF.Square), [r_cvx], [r_sq2])
                for half in range(2):
                    for g4 in range(4):
                        g = half * 4 + g4
                        for kk in range(2):
                            P.op(pe, lambda e: e.matmul(Q[3][:, g4 * 256:g4 * 256 + TM], ones_b, sq2[:, 2 * g + kk, :],
                                                        start=(kk == 0), stop=(kk == 1)),
                                 [r_const, r_sq2], [r_Q[3][g4 // 2]], sig=(kk == 1 and g4 % 2 == 1))
                    P.op(act, lambda e: e.activation(out=rg[:], in_=Q[3][:, :], func=AF.Ln,
                                                     bias=eps_c[:, 0:1], scale=1.0 / 256),
                         [r_Q[3][0], r_Q[3][1], r_const], [r_rg])
                    P.op(act, lambda e: e.activation(out=rg[:], in_=rg[:], func=AF.Exp, scale=-0.5), [r_rg], [r_rg])
                    for g4 in range(4):
                        for kk in range(2):
                            k = 2 * (half * 4 + g4) + kk
                            P.op(dve, lambda e: e.scalar_tensor_tensor(out=ynb[:, k, :], in0=cvx[:, k, :],
                                                                       scalar=onorm[:, k:k + 1],
                                                                       in1=rg[:, g4 * 256:g4 * 256 + TM],
                                                                       op0=ALU.mult, op1=ALU.mult),
                                 [r_cvx, r_par, r_rg], [r_ynb])
                nxt = load_o(0)
                for o in range(4):
                    wi = nxt
                    if o + 1 < 4:
                        nxt = load_o(o + 1)
                    elif ti + 1 < S // TM:
                        pre_in[0] = load_in(0)
                    oi = o % 2
                    for mm in range(2):
                        m = o * 2 + mm
                        pi = pcnt[0] % 2
                        pcnt[0] += 1
                        for kk in range(16):
                            P.op(pe, lambda e: e.matmul(py[pi][:, 0:TM], wo[wi][:, kk, mm * 128:(mm + 1) * 128],
                                                        ynb[:, kk, :], start=(kk == 0), stop=(kk == 15)),
                                 [r_wo[wi], r_ynb], [r_py[pi]], sig=(kk == 15))
                        P.op(dve, lambda e: e.tensor_tensor(xo[oi][:, mm, :], py[pi][:, 0:TM], xt[:, m, :], ALU.add),
                             [r_py[pi], ph.r_xT[0]], [r_xo[oi]])
                    P.dma(act, xres_tile(t0, TM, slice(o * 256, (o + 1) * 256)), xo[oi][:], [r_xo[oi]], [rx])
            end_phase(ph)

        def attention(j):
            l = NA + j
            ph = new_phase(T, nx=1)
            xt, xn = ph.xT[0], ph.xn
            NQ = T // 128
            sbp = lambda n, sh, dt: P.sb(n, sh, dt, ph.es)
            R = lambda n, dma=False: P.res(n, dma=dma)
            xnkv = sbp("xnkv", [128, KD, T], BF16)
            NWB = 3
            wbuf = [sbp("wbuf", [128, KD, 512], BF16) for i in range(NWB)]
            r_wbuf = [R("wbuf%d" % i, True) for i in range(NWB)]
            QTz = sbp("QTz", [128, 8, NQ, 2, 128], BF16)
            kT = sbp("kT", [128, 8, T], BF16)
            vt = sbp("vt", [128, NQ, 1024], BF16)
            NR = 6
            KTr = [sbp("KTr", [128, 8, 128], BF16) for i in range(NR)]
            Vr = [sbp("Vr", [128, 1024], BF16) for i in range(NR)]
            r_KTr = [R("KTr%d" % i, True) for i in range(NR)]
            r_Vr = [R("Vr%d" % i, True) for i in range(NR)]
            PT = [sbp("PT", [128, 5, 2, 128], BF16) for i in range(2)]
            r_PT = [R("PT0"), R("PT1")]
            Hk = sbp("Hk", [128, 16, 256], F32)
            Hkb = sbp("Hkb", [128, 16, 256], BF16)
            b0col = sbp("b0col", [128, 16], F32)
            gq = sbp("gq", [128, 2], F32)
            aT = sbp("aT", [128, 8, T], BF16)
            ksq = sbp("ksq", [128, 512], BF16)
            rk = sbp("rk", [128, 512], F32)
            rden = [sbp("rden", [128, 256], F32) for i in range(2)]
            xo = [sbp("xo", [128, 4, T], F32) for i in range(2)]
            r_xo = [R("xo0"), R("xo1")]
            r_par = R("par", True)
            r_vext = R("vext", True)
            r_kvd = R("kvd", True)
            r_xnkv, r_QTz, r_kT, r_vt, r_aT, r_ksq, r_rk, r_rden = [R(n) for n in (
                "xnkv", "QTz", "kT", "vt", "aT", "ksq", "rk", "rden0")]
            r_rden = [r_rden, R("rden1")]
            r_rden2 = [R("rden2a"), R("rden2b")]

            for hh in range(2):
                src = io["att_q_norm"][j, :]
                P.dma(sp, gq[hh * 64:(hh + 1) * 64, 0:1], bass.AP(src.tensor, src.offset, [[1, 64], [1, 1]]), [], [r_par])
                src = io["k_norm"][:]
                P.dma(sp, gq[hh * 64:(hh + 1) * 64, 1:2], bass.AP(src.tensor, src.offset, [[1, 64], [1, 1]]), [], [r_par])
                src = io["att_rel_bias"][j, :, 0:1]
                P.dma(sp, b0col[hh * 64:(hh + 1) * 64, :], bass.AP(src.tensor, src.offset, [[0, 64], [257, 16]]), [],
                      [r_par], allow_slow_non_contiguous=True)
            P.op(dve, lambda e: e.tensor_scalar(gq[:, 0:1], gq[:, 0:1], 0.125, None, ALU.mult), [r_par], [r_par])
            rb = io["att_rel_bias"][j]
            vtmp = sbp("vtmp", [16, 384], F32)
            r_vtmp = R("vtmp", True)
            P.dma(sp, vtmp[:, 127:383], rb[:, 0:256], [], [r_vtmp])
            P.op(pool, lambda e: e.memset(vtmp[:, 0:127], 0.0), [], [r_vtmp])
            P.op(dve, lambda e: e.tensor_scalar(vtmp[:, 128:383], vtmp[:, 128:383], vtmp[:, 127:128], None,
                                                ALU.subtract), [r_vtmp], [r_vtmp])
            P.op(dve, lambda e: e.memset(vtmp[:, 127:128], 0.0), [r_vtmp], [r_vtmp])
            P.dma(sp, vext[:, 0:383], vtmp[:, 0:383], [r_vtmp], [r_vext])
            for h8 in range(2):
                src = vext[h8 * 8:(h8 + 1) * 8, :]
                P.dma(sp, Hk[:, h8 * 8:(h8 + 1) * 8, :], bass.AP(src.tensor, src.offset, [[1, 128], [384, 8], [1, 256]]),
                      [r_vext], [r_par])
            P.op(dve, lambda e: e.tensor_copy(Hkb[:], Hk[:]), [r_par], [r_par])
            P.op(pool, lambda e: e.memset(QTz[:], 0.0), [], [r_QTz])
            for i in range(2):
                P.op(pool, lambda e: e.memset(PT[i][:], 0.0), [], [r_PT[i]])

            wctr = [0]
            pcnt = [0]

            wseq = []
            for ti_ in range(NT):
                if j == 0:
                    wseq += [(wb["w_kv"], 0), (wb["w_kv"], 512), (wb["w_kv"], 1024), (wb["w_kv"], 1536)]
                wseq += [(wb["att_w_q"][j], 0), (wb["att_w_q"][j], 512), (wb["att_w_o"][j], 0), (wb["att_w_o"][j], 512)]
            wissued = [0]

            def issue_w():
                if wissued[0] < len(wseq):
                    W, c0 = wseq[wissued[0]]
                    i = wissued[0] % NWB
                    P.dma(sp, wbuf[i][:], W[:, c0:c0 + 512].rearrange("(k p) c -> p k c", p=128), [r_wconv3[l][1]], [r_wbuf[i]])
                    wissued[0] += 1

            def load_w(W, c0):
                i = wctr[0] % NWB
                assert wseq[wctr[0]][1] == c0
                wctr[0] += 1
                while wissued[0] < min(wctr[0] + NWB - 1, len(wseq)):
                    issue_w()
                return i

            issue_w()
            issue_w()

            ksq4 = [ksq] + [sbp("ksq", [128, 512], BF16) for i in range(3)]
            rk4 = [rk] + [sbp("rk", [128, 512], F32) for i in range(3)]
            r_ksq4 = [r_ksq] + [R("ksq%d" % i) for i in range(1, 4)]
            r_rk4 = [r_rk] + [R("rk%d" % i) for i in range(1, 4)]

            def proj_heads(W, c0base, src, r_src, gcol, dest):
                for cg in range(2):
                    wi = load_w(W, c0base + cg * 512)
                    mains = [PS[:, i * 512:(i + 1) * 512] for i in range(4)]
                    stats = [PS[:, (4 + i) * 512:(5 + i) * 512] for i in range(4)]
                    for mm in range(4):
                        for k in range(KD):
                            P.op(pe, lambda e: e.matmul(mains[mm], wbuf[wi][:, k, mm * 128:(mm + 1) * 128], src[:, k, :],
                                                        start=(k == 0), stop=(k == KD - 1)),
                                 [r_wbuf[wi], r_src], [r_bank[mm]], sig=(k == KD - 1))
                    for mm in range(4):
                        P.op(act, lambda e: e.activation(out=ksq4[mm][:], in_=mains[mm], func=AF.Square),
                             [r_bank[mm]], [r_ksq4[mm]])
                    for mm in range(4):
                        P.op(pe, lambda e: e.matmul(stats[mm], cstb[:, 704:832], ksq4[mm][:], start=True, stop=True),
                             [r_ksq4[mm], r_const], [r_bank[4 + mm]])
                    for mm in range(4):
                        P.op(act, lambda e: e.activation(out=rk4[mm][:], in_=stats[mm], func=AF.Ln, bias=eps_c[:, 0:1],
                                                         scale=1.0 / 64), [r_bank[4 + mm], r_const], [r_rk4[mm]])
                    for mm in range(4):
                        P.op(act, lambda e: e.activation(out=rk4[mm][:], in_=rk4[mm][:], func=AF.Exp, scale=-0.5),
                             [r_rk4[mm]], [r_rk4[mm]])
                    for mm in range(4):
                        m = cg * 4 + mm
                        for (ps_, dst_, rd) in dest(m):
                            shp = (lambda a_: a_.rearrange("p (q c) -> p q c", c=128)) if len(dst_.shape) == 3 else (lambda a_: a_)
                            P.op(dve, lambda e: e.scalar_tensor_tensor(out=dst_, in0=shp(mains[mm][ps_, :]), scalar=gcol[ps_, :],
                                                                       in1=shp(rk4[mm][ps_, :]), op0=ALU.mult, op1=ALU.mult),
                                 [r_bank[mm], r_rk4[mm], r_par], [rd])

            for ti in range(NT):
                t0 = ti * T
                P.dma(sp, xt[:], xres_tile(t0, T), [r_xres[ti]], [ph.r_xT[0]])
                rms_stats(ph, 0)
                rms_apply(ph, 0, G_ATT + j, xn, ph.r_xn)
                if j == 0:
                    rms_apply(ph, 0, G_KV, xnkv, r_xnkv)
                    proj_heads(wb["w_kv"], 0, xnkv, r_xnkv, gq[:, 1:2],
                               lambda m: [(slice(0, 128), kT[:, m, :], r_kT)])
                    P.dma(sp, KTd[:, t0:t0 + T].rearrange("(m p) t -> p m t", p=128), kT[:], [r_kT], [r_kvd])
                    for cg in range(2):
                        wi = load_w(wb["w_kv"], 1024 + cg * 512)
                        for tb in range(NQ):
                            pi = pcnt[0] % 2
                            pcnt[0] += 1
                            for k in range(KD):
                                P.op(pe, lambda e: e.matmul(pu[pi], xnkv[:, k, tb * 128:(tb + 1) * 128], wbuf[wi][:, k, :],
                                                            start=(k == 0), stop=(k == KD - 1)),
                                     [r_wbuf[wi], r_xnkv], [r_pu[pi]], sig=(k == KD - 1))
                            P.op(act, lambda e: e.activation(out=vt[:, tb, cg * 512:(cg + 1) * 512], in_=pu[pi],
                                                             func=AF.Copy), [r_pu[pi]], [r_vt])
                    P.dma(sp, Vd[t0:t0 + T, :].rearrange("(tb p) c -> p tb c", p=128), vt[:], [r_vt], [r_kvd])
                proj_heads(wb["att_w_q"][j], 0, xn, ph.r_xn, gq[:, 0:1],
                           lambda m: [(slice(0, 64), QTz[0:64, m, :, 0, :], r_QTz),
                                      (slice(64, 128), QTz[64:128, m, :, 1, :], r_QTz)])
                def load_kv(KB):
                    sl = KB % NR
                    P.dma(sp, KTr[sl][:], KTd[:, KB * 128:(KB + 1) * 128].rearrange("(m p) t -> p m t", p=128),
                          [r_kvd], [r_KTr[sl]])
                    P.dma(sp, Vr[sl][:], Vd[KB * 128:(KB + 1) * 128, :], [r_kvd], [r_Vr[sl]])

                def s1(QB, qb, m, par, vb):
                    base = par * 1536
                    rs = [r_bank[par * 3 + i] for i in range(3)]
                    for jb in vb:
                        slot = (QB - 4 + jb) % NR
                        o_ = PS[:, base + jb * 256:base + (jb + 1) * 256]
                        last = (jb == vb[-1])
                        if jb < 3:
                            P.op(pe, lambda e: e.matmul(o_, KTr[slot][:, m, :], QTz[:, m, qb, :, :], start=True, stop=True),
                                 [r_KTr[slot], r_QTz], rs, sig=last)
                        else:
                            for hh in range(2):
                                oh = o_[:, hh * 128:(hh + 1) * 128]
                                P.op(pe, lambda e: e.matmul(oh, KTr[slot][:, m, :], QTz[:, m, qb, hh, :], start=True,
                                                            stop=False), [r_KTr[slot], r_QTz], rs, sig=False)
                                P.op(pe, lambda e: e.matmul(oh, Hkb[:, 2 * m + hh, (jb - 3) * 128:(jb - 2) * 128],
                                                            cstb[:, 128:256], start=False, stop=True),
                                     [r_par, r_const], rs, sig=(last and hh == 1))

                def s2(par, vb):
                    base = par * 1536
                    rs = [r_bank[par * 3 + i] for i in range(3)]
                    pt = PT[par]
                    sc = lambda p0, p1, jb, c0: mk(PS[p0:p1, base + jb * 256 + c0:base + jb * 256 + c0 + 1], [[128, 2], [1, 64]])
                    ex = lambda o_, i_: P.op(act, lambda e: e.activation(out=o_, in_=i_, func=AF.Exp), rs, [r_PT[par]])
                    if 0 in vb:
                        ex(pt[:, 0, :, 0:64], sc(0, 128, 0, 0))
                        ex(pt[64:128, 0, :, 64:128], sc(64, 128, 0, 64))
                    mid = [jb for jb in (1, 2, 3) if jb in vb]
                    if mid:
                        a, b = mid[0], mid[-1] + 1
                        ex(pt[:, a:b, :, :].rearrange("p j h q -> p (j h q)"), PS[:, base + a * 256:base + b * 256])
                    ex(pt[0:64, 4, :, 0:64], sc(0, 64, 4, 0))
                    ex(pt[:, 4, :, 64:128], sc(0, 128, 4, 64))

                def s3(QB, qb, m, par, vb):
                    ob = 6 + par
                    o_ = PS[:, ob * 512:ob * 512 + 256]
                    d_ = PS[:, ob * 512 + 256:ob * 512 + 512]
                    for i, jb in enumerate(vb):
                        slot = (QB - 4 + jb) % NR
                        P.op(pe, lambda e: e.matmul(o_, Vr[slot][:, m * 128:(m + 1) * 128], PT[par][:, jb, :, :],
                                                    start=(i == 0), stop=(i == len(vb) - 1)),
                             [r_Vr[slot], r_PT[par]], [r_bank[ob]], sig=False)
                    for i, jb in enumerate(vb):
                        P.op(pe, lambda e: e.matmul(d_, cstb[:, 832:960], PT[par][:, jb, :, :],
                                                    start=(i == 0), stop=(i == len(vb) - 1)),
                             [r_const, r_PT[par]], [r_bank[ob]], sig=(i == len(vb) - 1))
                    P.op(dve, lambda e: e.reciprocal(rden[par][:], d_), [r_bank[ob]], [r_rden[par]])
                    qc = slice(qb * 128, (qb + 1) * 128)
                    for hh in range(2):
                        ps_ = slice(hh * 64, (hh + 1) * 64)
                        P.op(dve, lambda e: e.tensor_tensor(aT[ps_, m, qc], o_[ps_, hh * 128:(hh + 1) * 128],
                                                            rden[par][ps_, hh * 128:(hh + 1) * 128], ALU.mult),
                             [r_bank[ob], r_rden[par]], [r_aT])

                seq = []
                for qb in range(NQ):
                    QB = ti * NQ + qb
                    vb = [jb for jb in range(5) if QB - 4 + jb >= 0]
                    for m in range(8):
                        seq.append((QB, qb, m, vb))
                if ti == 0 or j == 0:
                    load_kv(ti * NQ)
                for idx, (QB, qb, m, vb) in enumerate(seq):
                    par = idx % 2
                    if m == 0:
                        nb = QB + 1
                        if nb < S // 128 and (j == 1 or qb + 1 < NQ):
                            load_kv(nb)
                    if idx == 0:
                        s1(QB, qb, m, par, vb)
                    s2(par, vb)
                    if idx + 1 < len(seq):
                        QB2, qb2, m2, vb2 = seq[idx + 1]
                        s1(QB2, qb2, m2, 1 - par, vb2)
                    s3(QB, qb, m, par, vb)
                for o2 in range(2):
                    wi = load_w(wb["att_w_o"][j], o2 * 512)
                    oi = o2 % 2
                    for mm in range(4):
                        m = o2 * 4 + mm
                        pi = pcnt[0] % 2
                        pcnt[0] += 1
                        for k in range(KD):
                            P.op(pe, lambda e: e.matmul(py[pi], wbuf[wi][:, k, mm * 128:(mm + 1) * 128], aT[:, k, :],
                                                        start=(k == 0), stop=(k == KD - 1)),
                                 [r_wbuf[wi], r_aT], [r_py[pi]], sig=(k == KD - 1))
                        P.op(dve, lambda e: e.tensor_tensor(xo[oi][:, mm, :], py[pi], xt[:, m, :], ALU.add),
                             [r_py[pi], ph.r_xT[0]], [r_xo[oi]])
                    P.dma(act, xres_tile(t0, T, slice(o2 * 512, (o2 + 1) * 512)), xo[oi][:], [r_xo[oi]], [r_xres[ti]])
            end_phase(ph)

        prologue()
        for l in range(L):
            if "noffn" not in cfg.mode:
                ffn(l, 0)
            if l < NA:
                if "nomix" not in cfg.mode:
                    mamba(l)
            else:
                if "nomix" not in cfg.mode:
                    attention(l - NA)
            if "noffn" not in cfg.mode:
                ffn(l, 1)
        epilogue()
        P.finish(sp)
    return nc


_CACHE = {}


def kernel(**inputs):
    S = inputs["x"].shape[1]
    cfg = Cfg(S)
    key = (S,)
    if key not in _CACHE:
        _CACHE[key] = build(cfg)
    nc = _CACHE[key]
    consts = make_consts()
    shared = {k: np.ascontiguousarray(np.asarray(v, dtype=np.float32)) for k, v in inputs.items() if k != "x"}
    xin = np.asarray(inputs["x"], dtype=np.float32)
    in_maps = []
    for c in range(NCORES):
        m = {"x": np.ascontiguousarray(xin[c])}
        m.update(shared)
        m.update(consts)
        in_maps.append(m)
    res = run_bass_kernel_spmd(nc, in_maps, core_ids=list(range(NCORES)))
    return np.stack([np.asarray(r["out"]) for r in res.results], axis=0).astype(np.float32)
```

```python
import contextlib
import numpy as np
import concourse.bass as bass
import concourse.mybir as mybir
from concourse.bass_utils import run_bass_kernel_spmd

F32 = mybir.dt.float32
BF16 = mybir.dt.bfloat16
AF = mybir.ActivationFunctionType
ALU = mybir.AluOpType

D = 1024
KD = D // 128
DFF = 2816
KF = DFF // 128
EPS = 1e-6
NCORES = 8


class Sem:
    def __init__(self, h, dma):
        self.h = h
        self.dma = dma
        self.issued = 0


class Res:
    __slots__ = ("name", "w", "r", "dsem")

    def __init__(self, name, dsem=None):
        self.name = name
        self.w = None
        self.r = {}
        self.dsem = dsem


class Eng:
    def __init__(self, name, h, sem):
        self.name = name
        self.h = h
        self.sem = sem
        self.waited = {}


class Prog:
    def __init__(self, nc, es):
        self.nc = nc
        self.es = es
        self.nsem = 0
        self.pe = Eng("pe", nc.tensor, self.new_sem(False))
        self.act = Eng("act", nc.scalar, self.new_sem(False))
        self.dve = Eng("dve", nc.vector, self.new_sem(False))
        self.pool = Eng("pool", nc.gpsimd, self.new_sem(False))
        self.sp = Eng("sp", nc.sync, self.new_sem(False))
        self.engs = [self.pe, self.act, self.dve, self.pool, self.sp]

    def new_sem(self, dma):
        self.nsem += 1
        if not hasattr(self, "sems"):
            self.sems = []
        sm = Sem(self.es.enter_context(self.nc.semaphore("s%d" % self.nsem)), dma)
        self.sems.append(sm)
        return sm
        return Sem(self.es.enter_context(self.nc.semaphore("s%d" % self.nsem)), dma)

    def res(self, name, dma=False, share=None):
        if share is not None:
            return Res(name, share)
        if dma and getattr(self, "in_phase", False):
            if self.pool_idx == len(self.pool_list):
                self.pool_list.append(self.new_sem(True))
            sm = self.pool_list[self.pool_idx]
            self.pool_idx += 1
            return Res(name, sm)
        return Res(name, self.new_sem(True) if dma else None)

    def phase(self, on):
        if not hasattr(self, "pool_list"):
            self.pool_list = []
        self.pool_idx = 0
        self.in_phase = on

    def sb(self, name, shape, dt, es=None):
        self.nname = getattr(self, "nname", 0) + 1
        return (es or self.es).enter_context(
            self.nc.sbuf_tensor("sb_%s_%d" % (name, self.nname), list(shape), dt))

    def barrier(self):
        for e in self.engs:
            for sm in self.sems:
                if sm is e.sem or not sm.issued:
                    continue
                if e.waited.get(sm, 0) < sm.issued:
                    e.h.wait_ge(sm.h, sm.issued)
                    e.waited[sm] = sm.issued

    def ps(self, name, shape, dt=F32):
        return self.es.enter_context(self.nc.psum_tensor("ps_" + name, list(shape), dt))

    def _need(self, eng, stamp):
        if stamp is None:
            return
        sem, val = stamp
        if sem.dma:
            val = sem.issued
        elif sem is eng.sem:
            if eng.name == "pe" or val > sem.issued:
                return
        if eng.waited.get(sem, 0) >= val:
            return
        eng.h.wait_ge(sem.h, val)
        eng.waited[sem] = val

    def _deps(self, eng, reads, writes):
        for r in reads:
            self._need(eng, r.w)
        for w in writes:
            self._need(eng, w.w)
            for s, v in w.r.items():
                self._need(eng, (s, v))

    def op(self, eng, fn, reads=(), writes=(), sig=True):
        self._deps(eng, reads, writes)
        ins = fn(eng.h)
        if sig:
            eng.sem.issued += 1
            ins.then_inc(eng.sem.h, 1)
            stamp = (eng.sem, eng.sem.issued)
        else:
            stamp = (eng.sem, eng.sem.issued + 1)
        for r in reads:
            r.r[stamp[0]] = stamp[1]
        for w in writes:
            w.w = stamp
            w.r = {}
        return ins

    def dma(self, q, out, in_, reads, writes, **kw):
        self._deps(q, reads, writes)
        wres = writes[0]
        sem = wres.dsem
        ins = q.h.dma_start(out=out, in_=in_, **kw)
        sem.issued += 16
        ins.then_inc(sem.h, 16)
        stamp = (sem, sem.issued)
        for r in reads:
            r.r[sem] = sem.issued
        for w in writes:
            w.w = stamp
            w.r = {}
        return ins

    def finish(self, eng):
        for sem in self.sems:
            if sem.dma and sem.issued:
                self._need(eng, (sem, sem.issued))


DI = 2048
NH = 32
DS = 128
DINP = 6176
NCOLC = 960


class Cfg:
    def __init__(self, S, T=512, TM=256, n_ssm=2, n_att=2, mode="full"):
        self.S = S
        self.T = T
        self.TM = TM
        self.NT = S // T
        self.n_ssm = n_ssm
        self.n_att = n_att
        self.depth = n_ssm + n_att
        self.mode = mode


def make_consts():
    c = np.zeros((128, NCOLC), np.float32)
    c[:, 0:128] = np.eye(128)
    c[:, 128:256] = np.eye(128)[::-1]
    a = np.arange(64)
    c[0:64, 256:320] = (a[:, None] <= a[None, :])
    c[0:64, 320:384] = (a[:, None] > a[None, :])
    c[0:64, 384:448] = np.where(a[None, :] < a[:, None], -30000.0, 30000.0)
    c[:, 448:512] = 1.0
    c[:, 576 + 64:704] = 1.0
    p = np.arange(128)
    c[:, 704:832] = ((p[:, None] // 64) == (p[None, :] // 64))
    c[:, 832:960] = 1.0
    return {"cst": c}


def declare_inputs(nc, cfg):
    S = cfg.S
    L, NA, NB = cfg.depth, cfg.n_ssm, cfg.n_att
    t = {}

    def inp(name, shape, dt=F32):
        t[name] = nc.dram_tensor(name, list(shape), dt, kind="ExternalInput").ap()

    inp("x", [S, D])
    for f in ("ffn1", "ffn2"):
        inp(f + "_norm", [L, D])
        inp(f + "_w_gate", [L, D, DFF])
        inp(f + "_w_up", [L, D, DFF])
        inp(f + "_w_down", [L, DFF, D])
    if NA:
        inp("ssm_norm", [NA, D])
        inp("ssm_in_proj", [NA, D, DINP])
        inp("ssm_conv_w", [NA, 4, 4096])
        inp("ssm_conv_b", [NA, 4096])
        inp("ssm_dt_bias", [NA, NH])
        inp("ssm_A_log", [NA, NH])
        inp("ssm_D", [NA, NH])
        inp("ssm_out_norm", [NA, DI])
        inp("ssm_out_proj", [NA, DI, D])
    if NB:
        inp("kv_norm", [D])
        inp("w_kv", [D, 2 * D])
        inp("k_norm", [64])
        inp("att_norm", [NB, D])
        inp("att_w_q", [NB, D, D])
        inp("att_q_norm", [NB, 64])
        inp("att_rel_bias", [NB, 16, 257])
        inp("att_w_o", [NB, D, D])
    inp("cst", [128, NCOLC])
    t["out"] = nc.dram_tensor("out", [S, D], F32, kind="ExternalOutput").ap()
    return t


def mk(base, dims):
    return bass.AP(base.tensor, base.offset, [list(base.ap[0])] + [list(d) for d in dims])


def build(cfg):
    nc = bass.Bass("TRN2", target_bir_lowering=False)
    io = declare_inputs(nc, cfg)
    S, T, NT, L = cfg.S, cfg.T, cfg.NT, cfg.depth
    NA, NB = cfg.n_ssm, cfg.n_att

    def dram(name, shape, dt):
        return nc.dram_tensor(name, list(shape), dt, kind="Internal").ap()

    xres = dram("xres", [D, S], F32)
    wb = {}
    for f in ("ffn1", "ffn2"):
        wb[f + "_w_gate"] = dram(f + "_wg_b", [L, D, DFF], BF16)
        wb[f + "_w_up"] = dram(f + "_wu_b", [L, D, DFF], BF16)
        wb[f + "_w_down"] = dram(f + "_wd_b", [L, DFF, D], BF16)
    if NA:
        wb["ssm_in_proj"] = dram("ssm_in_b", [NA, D, DINP], BF16)
        wb["ssm_out_proj"] = dram("ssm_out_b", [NA, DI, D], BF16)
    if NB:
        wb["w_kv"] = dram("w_kv_b", [D, 2 * D], BF16)
        wb["att_w_q"] = dram("att_wq_b", [NB, D, D], BF16)
        wb["att_w_o"] = dram("att_wo_b", [NB, D, D], BF16)
        KTd = dram("KTd", [D, S], BF16)
        Vd = dram("Vd", [S, D], BF16)
        vext = dram("vext", [16, 384], F32)
    if NA:
        acsd = dram("acsd", [2, 128, 64], F32)

    with contextlib.ExitStack() as es:
        P = Prog(nc, es)
        pe, act, dve, pool, sp = P.pe, P.act, P.dve, P.pool, P.sp

        cst = P.sb("cst", [128, NCOLC], F32)
        cstb = P.sb("cstb", [128, NCOLC], BF16)
        ident = cst[:, 0:128]
        ones_b = cstb[:, 832:960]
        NG_ = 2 * L + NA + NB + 1
        gains = P.sb("gains", [128, NG_, KD], F32)
        eps_c = P.sb("eps_c", [128, 2], F32)
        r_ld = P.res("const_ld", dma=True)
        P.dma(sp, cst[:], io["cst"][:, :], [], [r_ld])
        gsrc = []
        for f in ("ffn1", "ffn2"):
            for l in range(L):
                gsrc.append(io[f + "_norm"][l, :])
        for l in range(NA):
            gsrc.append(io["ssm_norm"][l, :])
        for l in range(NB):
            gsrc.append(io["att_norm"][l, :])
        if NB:
            gsrc.append(io["kv_norm"][:])
        for i, g in enumerate(gsrc):
            P.dma(sp, gains[:, i, :], g.rearrange("(k p) -> p k", p=128), [], [r_ld],
                  allow_slow_non_contiguous=True)
        G_SSM, G_ATT, G_KV = 2 * L, 2 * L + NA, 2 * L + NA + NB
        r_const = P.res("const")
        P.op(dve, lambda e: e.tensor_copy(cstb[:], cst[:]), [r_ld], [r_const])
        P.op(dve, lambda e: e.memset(eps_c[:, 0:1], EPS), [], [r_const])
        P.op(dve, lambda e: e.memset(eps_c[:, 1:2], 1.0), [], [r_const])
        r_ones = r_const
        r_gains = r_const

        r_wconv = [P.res("wconv%d" % l, dma=True) for l in range(L)]

        def conv_w(src, dst, r):
            rows = src.shape[0]
            for r0 in range(0, rows, 256):
                P.dma(pool, dst[r0:r0 + 256, :], src[r0:r0 + 256, :], [], [r])

        for l in range(L):
            for nm in ("_w_gate", "_w_up", "_w_down"):
                conv_w(io["ffn1" + nm][l], wb["ffn1" + nm][l], r_wconv[l])
            if l < NA:
                conv_w(io["ssm_in_proj"][l], wb["ssm_in_proj"][l], r_wconv[l])
                conv_w(io["ssm_out_proj"][l], wb["ssm_out_proj"][l], r_wconv[l])
            else:
                j = l - NA
                if j == 0:
                    conv_w(io["w_kv"], wb["w_kv"], r_wconv[l])
                conv_w(io["att_w_q"][j], wb["att_w_q"][j], r_wconv[l])
                conv_w(io["att_w_o"][j], wb["att_w_o"][j], r_wconv[l])
            for nm in ("_w_gate", "_w_up", "_w_down"):
                conv_w(io["ffn2" + nm][l], wb["ffn2" + nm][l], r_wconv[l])

        r_xres = [P.res("xres%d" % i, dma=True) for i in range(NT)]
        r_out = P.res("out", dma=True)
        PS = P.ps("PS", [128, 4096])
        r_bank = [P.res("bank%d" % i) for i in range(8)]
        Q = [PS[:, i * 1024:(i + 1) * 1024] for i in range(4)]
        r_Q = [[r_bank[2 * i + h] for h in range(2)] for i in range(4)]
        pg = [Q[0][:, 0:512], Q[0][:, 512:1024]]
        pu = [Q[1][:, 0:512], Q[1][:, 512:1024]]
        py = [Q[2][:, 0:512], Q[2][:, 512:1024]]
        pst = Q[3][:, 0:512]
        ptr = Q[3][:, 512:1024]
        r_pg, r_pu, r_py = r_Q[0], r_Q[1], r_Q[2]
        r_pst, r_ptr = r_Q[3][0], r_Q[3][1]

        class Ph:
            pass

        def new_phase(Tt, nx=2):
            P.barrier()
            P.phase(True)
            ph = Ph()
            ph.es = contextlib.ExitStack()
            ph.T = Tt
            ph.xT = [P.sb("xT", [128, KD, Tt], F32, ph.es) for i in range(nx)]
            ph.r_xT = [P.res("xT%d" % i, dma=True) for i in range(nx)]
            ph.xn = P.sb("xn", [128, KD, Tt], BF16, ph.es)
            ph.r_xn = P.res("xn")
            ph.sq = P.sb("sq", [128, KD, Tt], BF16, ph.es)
            ph.r_sq = P.res("sq")
            ph.rstd = P.sb("rstd", [128, Tt], F32, ph.es)
            ph.r_rstd = P.res("rstd")
            return ph

        def end_phase(ph):
            P.barrier()
            P.phase(False)
            ph.es.close()

        def xres_tile(t0, Tt, rows=slice(0, D)):
            return xres[rows, t0:t0 + Tt].rearrange("(k p) t -> p k t", p=128)

        def prologue():
            ph = new_phase(T)
            NTM_ = 4
            tm = [P.sb("tm", [128, D], F32, ph.es) for i in range(NTM_)]
            r_tm = [P.res("tm%d" % i, dma=True) for i in range(NTM_)]
            nblk = S // 128
            for b in range(min(NTM_ - 1, nblk)):
                P.dma(sp, tm[b % NTM_][:], io["x"][b * 128:(b + 1) * 128, :], [], [r_tm[b % NTM_]])
            bk = 0
            for b in range(nblk):
                i = b % NTM_
                if b + NTM_ - 1 < nblk:
                    nb_ = b + NTM_ - 1
                    P.dma(sp, tm[nb_ % NTM_][:], io["x"][nb_ * 128:(nb_ + 1) * 128, :], [], [r_tm[nb_ % NTM_]])
                ti = b // (T // 128)
                xt = ph.xT[ti % 2]
                col = (b % (T // 128)) * 128
                for k4 in range(0, KD, 4):
                    bank = bk % 8
                    bk += 1
                    pb = PS[:, bank * 512:(bank + 1) * 512]
                    for k in range(k4, k4 + 4):
                        P.op(pe, lambda e: e.transpose(pb[:, (k - k4) * 128:(k - k4 + 1) * 128],
                                                       tm[i][:, k * 128:(k + 1) * 128], ident),
                             [r_tm[i], r_const], [r_bank[bank]], sig=(k == k4 + 3))
                    P.op(act, lambda e: e.activation(
                        out=xt[:, k4:k4 + 4, col:col + 128],
                        in_=pb.rearrange("p (k c) -> p k c", k=4), func=AF.Copy),
                        [r_bank[bank]], [ph.r_xT[ti % 2]])
                if (b + 1) % (T // 128) == 0:
                    P.dma(act, xres_tile(ti * T, T), xt[:], [ph.r_xT[ti % 2]], [r_xres[ti]])
            end_phase(ph)

        def epilogue():
            ph = new_phase(T)
            NTM_ = 4
            tm = [P.sb("tm", [128, D], F32, ph.es) for i in range(NTM_)]
            r_tm = [P.res("tm%d" % i, dma=True) for i in range(NTM_)]
            P.dma(sp, ph.xT[0][:], xres_tile(0, T), [r_xres[0]], [ph.r_xT[0]])
            bk = 0
            for ti in range(NT):
                xt = ph.xT[ti % 2]
                if ti + 1 < NT:
                    P.dma(sp, ph.xT[(ti + 1) % 2][:], xres_tile((ti + 1) * T, T), [r_xres[ti + 1]],
                          [ph.r_xT[(ti + 1) % 2]])
                for bb in range(T // 128):
                    b = ti * (T // 128) + bb
                    i = b % NTM_
                    for k4 in range(0, KD, 4):
                        bank = bk % 8
                        bk += 1
                        pb = PS[:, bank * 512:(bank + 1) * 512]
                        for k in range(k4, k4 + 4):
                            P.op(pe, lambda e: e.transpose(
                                pb[:, (k - k4) * 128:(k - k4 + 1) * 128],
                                xt[:, k, bb * 128:(bb + 1) * 128], ident),
                                [ph.r_xT[ti % 2], r_const], [r_bank[bank]], sig=(k == k4 + 3))
                        P.op(act, lambda e: e.activation(out=tm[i][:, k4 * 128:(k4 + 4) * 128],
                                                         in_=pb, func=AF.Copy),
                             [r_bank[bank]], [r_tm[i]])
                    P.dma(act, io["out"][b * 128:(b + 1) * 128, :], tm[i][:], [r_tm[i]], [r_out])
            end_phase(ph)

        def rms_sq(ph, slot):
            xt = ph.xT[slot]
            P.op(act, lambda e: e.activation(out=ph.sq[:], in_=xt[:], func=AF.Square),
                 [ph.r_xT[slot]], [ph.r_sq])

        def rms_stats(ph, slot, do_sq=True):
            xt = ph.xT[slot]
            Tt = ph.T
            if do_sq:
                rms_sq(ph, slot)
            for s0 in range(0, Tt, 512):
                w = min(512, Tt - s0)
                cs = slice(s0, s0 + w)
                for k in range(KD):
                    P.op(pe, lambda e: e.matmul(pst[:, 0:w], ones_b, ph.sq[:, k, cs],
                                                start=(k == 0), stop=(k == KD - 1)),
                         [r_ones, ph.r_sq], [r_pst], sig=(k == KD - 1))
                P.op(act, lambda e: e.activation(out=ph.rstd[:, cs], in_=pst[:, 0:w], func=AF.Ln,
                                                 bias=eps_c[:, 0:1], scale=1.0 / D),
                     [r_pst, r_const], [ph.r_rstd])
                P.op(act, lambda e: e.activation(out=ph.rstd[:, cs], in_=ph.rstd[:, cs], func=AF.Exp, scale=-0.5),
                     [ph.r_rstd], [ph.r_rstd])

        def rms_apply(ph, slot, gidx, xn_t, r_xn_t):
            xt = ph.xT[slot]
            for k in range(KD):
                P.op(dve, lambda e: e.scalar_tensor_tensor(
                    out=xn_t[:, k, :], in0=xt[:, k, :], scalar=gains[:, gidx, k:k + 1], in1=ph.rstd[:],
                    op0=ALU.mult, op1=ALU.mult),
                    [ph.r_xT[slot], r_gains, ph.r_rstd], [r_xn_t])

        def ffn(l, fidx):
            ph = new_phase(T)
            NS = T // 512
            f = ("ffn1", "ffn2")[fidx]
            Wg, Wu, Wd = wb[f + "_w_gate"][l], wb[f + "_w_up"][l], wb[f + "_w_down"][l]
            gidx = fidx * L + l
            GH, GO = 2, 2
            NG, NO = KF // GH, KD // GO
            NB3 = 3
            wg = [P.sb("wg", [128, KD, GH * 128], BF16, ph.es) for i in range(NB3)]
            wu = [P.sb("wu", [128, KD, GH * 128], BF16, ph.es) for i in range(NB3)]
            r_wg = [P.res("wg%d" % i, dma=True) for i in range(NB3)]
            r_wu = [P.res("wu%d" % i, dma=True) for i in range(NB3)]
            wd = [P.sb("wd", [128, KF, GO * 128], BF16, ph.es) for i in range(3)]
            r_wd = [P.res("wd%d" % i, dma=True) for i in range(3)]
            hact = P.sb("hact", [128, KF, T], BF16, ph.es)
            r_hact = [P.res("hact%d" % j) for j in range(KF)]
            sg = [P.sb("sg", [128, 512], F32, ph.es) for i in range(2)]
            r_sg = [P.res("sg%d" % i) for i in range(2)]
            xo = [P.sb("xo", [128, GO, T], F32, ph.es) for i in range(2)]
            r_xo = [P.res("xo%d" % i, dma=True) for i in range(2)]
            wctr = [0, 0]

            def load_gu(g):
                i = wctr[0] % NB3
                wctr[0] += 1
                cs = slice(g * GH * 128, (g + 1) * GH * 128)
                P.dma(sp, wg[i][:], Wg[:, cs].rearrange("(k p) c -> p k c", p=128), [r_wconv[l]], [r_wg[i]])
                P.dma(sp, wu[i][:], Wu[:, cs].rearrange("(k p) c -> p k c", p=128), [r_wconv[l]], [r_wu[i]])
                return i

            def load_d(o):
                i = wctr[1] % 3
                wctr[1] += 1
                cs = slice(o * GO * 128, (o + 1) * GO * 128)
                P.dma(sp, wd[i][:], Wd[:, cs].rearrange("(k p) c -> p k c", p=128), [r_wconv[l]], [r_wd[i]])
                return i

            pcnt = [0, 0]
            xn2 = [ph.xn, P.sb("xn2", [128, KD, T], BF16, ph.es)]
            r_xn2 = [ph.r_xn, P.res("xn2")]

            def norm_a(ti):
                P.dma(sp, ph.xT[ti % 2][:], xres_tile(ti * T, T), [r_xres[ti]], [ph.r_xT[ti % 2]])
                rms_sq(ph, ti % 2)

            def norm_b(ti):
                rms_stats(ph, ti % 2, do_sq=False)
                rms_apply(ph, ti % 2, gidx, xn2[ti % 2], r_xn2[ti % 2])

            norm_a(0)
            norm_b(0)
            pre_gu = load_gu(0)
            for ti in range(NT):
                slot = ti % 2
                xt = ph.xT[slot]
                xn = xn2[slot]
                r_xn_t = r_xn2[slot]
                gq_ = [pre_gu, load_gu(1)]
                pre_d = None
                for g in range(NG):
                    if g == NG - 3:
                        pre_d = load_d(0)
                    if ti + 1 < NT and g == 1:
                        norm_a(ti + 1)
                    if ti + 1 < NT and g == 6:
                        norm_b(ti + 1)
                    wi = gq_.pop(0)
                    if g + 2 < NG:
                        gq_.append(load_gu(g + 2))
                    for jj in range(GH):
                        j = g * GH + jj
                        for s in range(NS):
                            pi = pcnt[0] % 2
                            pcnt[0] += 1
                            cs = slice(s * 512, (s + 1) * 512)
                            for k in range(KD):
                                P.op(pe, lambda e: e.matmul(pg[pi], wg[wi][:, k, jj * 128:(jj + 1) * 128],
                                                            xn[:, k, cs], start=(k == 0), stop=(k == KD - 1)),
                                     [r_wg[wi], r_xn_t], [r_pg[pi]], sig=(k == KD - 1))
                            for k in range(KD):
                                P.op(pe, lambda e: e.matmul(pu[pi], wu[wi][:, k, jj * 128:(jj + 1) * 128],
                                                            xn[:, k, cs], start=(k == 0), stop=(k == KD - 1)),
                                     [r_wu[wi], r_xn_t], [r_pu[pi]], sig=(k == KD - 1))
                            P.op(act, lambda e: e.activation(out=sg[pi][:], in_=pg[pi], func=AF.Silu),
                                 [r_pg[pi]], [r_sg[pi]])
                            P.op(dve, lambda e: e.tensor_tensor(hact[:, j, cs], sg[pi][:], pu[pi], ALU.mult),
                                 [r_sg[pi], r_pu[pi]], [r_hact[j]])
                dq_ = [pre_d, load_d(1)]
                for o in range(NO):
                    wi = dq_.pop(0)
                    if o + 2 < NO:
                        dq_.append(load_d(o + 2))
                    if o == 2 and ti + 1 < NT:
                        pre_gu = load_gu(0)
                    oi = o % 2
                    for mm in range(GO):
                        m = o * GO + mm
                        for s in range(NS):
                            pi = pcnt[1] % 2
                            pcnt[1] += 1
                            cs = slice(s * 512, (s + 1) * 512)
                            for j in range(KF):
                                P.op(pe, lambda e: e.matmul(py[pi], wd[wi][:, j, mm * 128:(mm + 1) * 128],
                                                            hact[:, j, cs], start=(j == 0), stop=(j == KF - 1)),
                                     [r_wd[wi], r_hact[j]], [r_py[pi]], sig=(j == KF - 1))
                            P.op(dve, lambda e: e.scalar_tensor_tensor(
                                out=xo[oi][:, mm, cs], in0=py[pi], scalar=0.5, in1=xt[:, m, cs],
                                op0=ALU.mult, op1=ALU.add),
                                [r_py[pi], ph.r_xT[slot]], [r_xo[oi]])
                    P.dma(act, xres_tile(ti * T, T, slice(o * GO * 128, (o + 1) * GO * 128)), xo[oi][:],
                          [r_xo[oi], ph.r_xT[slot]], [r_xres[ti]])
            end_phase(ph)

        def mamba(l):
            TM = cfg.TM
            CH = TM // 64
            ph = new_phase(TM, nx=1)
            xt, xn = ph.xT[0], ph.xn
            Win = wb["ssm_in_proj"][l]
            Wout = wb["ssm_out_proj"][l]
            sbp = lambda n, sh, dt: P.sb(n, sh, dt, ph.es)
            convw = sbp("convw", [128, 32, 4], F32)
            convb = sbp("convb", [128, 32], F32)
            onorm = sbp("onorm", [128, 16], F32)
            Dcol = sbp("Dcol", [128, 16], F32)
            dtb = sbp("dtb", [64, 32], F32)
            Abc = sbp("Abc", [64, 32], F32)
            wdt = sbp("wdt", [128, KD, 32], BF16)
            halo = sbp("halo", [128, 32, 3], F32)
            st = sbp("st", [128, 32, 64], F32)
            stbp = sbp("stbp", [128, 32, 128], BF16)
            Xp = sbp("Xp", [64, 32, 128], BF16)
            Xd = [sbp("Xd", [64, 32, 64], BF16) for i in range(2)]
            cvx = sbp("cvx", [128, 16, TM], F32)
            zs = sbp("zs", [128, 16, TM], BF16)
            ynb = sbp("ynb", [128, 16, TM], BF16)
            BT = sbp("BT", [128, 8, TM], BF16)
            CT = sbp("CT", [128, 8, TM], BF16)
            raw = [sbp("raw", [128, TM + 3], F32) for i in range(2)]
            acc = [sbp("acc", [128, TM], F32) for i in range(2)]
            NWI = 3
            winp = [sbp("winp", [128, KD, 512], BF16) for i in range(NWI)]
            wo = [winp[i][:].rearrange("p k c -> p (k c)").rearrange("p (k c) -> p k c", k=16) for i in range(NWI)]
            xo = [sbp("xo", [128, 2, TM], F32) for i in range(2)]
            dtt = sbp("dtt", [64, CH * 32], F32)
            dte = sbp("dte", [64, CH * 32], F32)
            dtA = sbp("dtA", [64, CH * 32], F32)
            acs = sbp("acs", [64, CH * 32], F32)
            dec = sbp("dec", [64, CH * 32], F32)
            dtd = sbp("dtd", [64, CH * 32], F32)
            cdec = sbp("cdec", [128, CH * 32], F32)
            AcsRowB = [sbp("AcsRowB", [128, 32, 64], F32) for i in range(2)]
            dseg = sbp("dseg", [64, 32, 64], F32)
            WT = sbp("WT", [64, 32, 64], BF16)
            Btok = [sbp("Btok", [64, 8, 128], BF16) for i in range(2)]
            eA2 = [sbp("eA2", [128, 1024], F32) for i in range(2)]
            ydsb = [sbp("ydsb", [128, 1024], F32) for i in range(2)]
            tY = sbp("tY", [128, 1024], F32)
            ybuf = sbp("ybuf", [128, 16, TM], F32)
            acsT = sbp("acsT", [128, 64], F32)
            cbs = sbp("cbs", [64, 8, 64], F32)
            r_cbs = P.res("cbs")
            r_WT2 = P.res("WT2")
            r_dsegh = [P.res("dseg0"), P.res("dseg1")]
            R = lambda n, dma=False: P.res(n, dma=dma)
            r_par = R("par", True)
            r_halo, r_st, r_stbp, r_Xp, r_cvx, r_zs, r_ynb = [R(n) for n in (
                "halo", "st", "stbp", "Xp", "cvx", "zs", "ynb")]
            sq2, r_sq2 = zs, r_zs
            r_BT, r_CT = R("BT"), R("CT")
            r_raw = [R("raw0"), R("raw1")]
            r_acc = [R("acc0"), R("acc1")]
            r_winp = [R("winp%d" % i, True) for i in range(NWI)]
            r_wo = r_winp
            r_xo = [R("xo0"), R("xo1")]
            r_dt = R("dt")
            r_WT, r_tY, r_dseg, r_ybuf, r_acsT = [R(n) for n in ("WT", "tY", "dseg", "ybuf", "acsT")]
            rg, r_rg = tY, r_tY
            r_Btok = [R("Btok0"), R("Btok1")]
            r_eA2 = [R("eA20"), R("eA21")]
            r_ydsb = [R("ydsb0"), R("ydsb1")]
            r_Xd = [R("Xd0"), R("Xd1")]
            r_ARB = [R("ARB0", True), R("ARB1", True)]
            r_acsd = [R("acsd0", True), R("acsd1", True)]

            for t in range(4):
                for j0 in range(0, 32, 8):
                    P.dma(sp, convw[:, j0:j0 + 8, t],
                          io["ssm_conv_w"][l, t, j0 * 128:(j0 + 8) * 128].rearrange("(j p) -> p j", p=128), [], [r_par],
                          allow_slow_non_contiguous=True)
            for j0 in range(0, 32, 8):
                P.dma(sp, convb[:, j0:j0 + 8],
                      io["ssm_conv_b"][l, j0 * 128:(j0 + 8) * 128].rearrange("(j p) -> p j", p=128), [], [r_par],
                      allow_slow_non_contiguous=True)
            for j0 in range(0, 16, 8):
                P.dma(sp, onorm[:, j0:j0 + 8],
                      io["ssm_out_norm"][l, j0 * 128:(j0 + 8) * 128].rearrange("(j p) -> p j", p=128), [], [r_par],
                      allow_slow_non_contiguous=True)
            dsrc = io["ssm_D"][l, :]
            for hh in range(2):
                P.dma(sp, Dcol[hh * 64:(hh + 1) * 64, :],
                      bass.AP(dsrc.tensor, dsrc.offset + hh, [[0, 64], [2, 16]]), [], [r_par],
                      allow_slow_non_contiguous=True)
            for dst_, nm in ((dtb, "ssm_dt_bias"), (Abc, "ssm_A_log")):
                src = io[nm][l, :]
                P.dma(sp, dst_[:], bass.AP(src.tensor, src.offset, [[0, 64], [1, 32]]), [], [r_par])
            P.dma(sp, wdt[:], Win[:, 6144:6176].rearrange("(k p) c -> p k c", p=128), [r_wconv[l]], [r_par])
            P.op(act, lambda e: e.activation(out=Abc[:], in_=Abc[:], func=AF.Exp), [r_par], [r_par])
            P.op(dve, lambda e: e.tensor_scalar(Abc[:], Abc[:], -1.0, None, ALU.mult), [r_par], [r_par])
            P.op(pool, lambda e: e.memset(halo[:], 0.0), [], [r_halo])
            P.op(pool, lambda e: e.memset(st[:], 0.0), [], [r_st])
            P.op(pool, lambda e: e.memset(stbp[:], 0.0), [], [r_stbp])
            P.op(pool, lambda e: e.memset(Xp[:], 0.0), [], [r_Xp])

            wctr = [0, 0]

            def load_in(g):
                i = wctr[0] % NWI
                wctr[0] += 1
                P.dma(sp, winp[i][:], Win[:, g * 512:(g + 1) * 512].rearrange("(k p) c -> p k c", p=128),
                      [r_wconv[l]], [r_winp[i]])
                return i

            def load_o(o):
                i = wctr[0] % NWI
                wctr[0] += 1
                P.dma(sp, wo[i], Wout[:, o * 256:(o + 1) * 256].rearrange("(k p) c -> p k c", p=128),
                      [r_wconv[l]], [r_wo[i]])
                return i

            def padded(t, hf):
                b = t[:, hf * 16, 0:1]
                return mk(b, [[256, 8], [192, 2], [1, 64]])

            pcnt = [0]
            pre_in = [None]
            for ti in range(S // TM):
                t0 = ti * TM
                rx = r_xres[t0 // T]
                P.dma(sp, xt[:], xres_tile(t0, TM), [rx], [ph.r_xT[0]])
                rms_stats(ph, 0)
                rms_apply(ph, 0, G_SSM + l, xn, ph.r_xn)
                N_ = CH * 32
                for c in range(CH):
                    for k in range(KD):
                        P.op(pe, lambda e: e.matmul(Q[3][0:64, c * 32:(c + 1) * 32], xn[:, k, c * 64:(c + 1) * 64],
                                                    wdt[:, k, :], start=(k == 0), stop=(k == KD - 1)),
                             [ph.r_xn, r_par], [r_Q[3][0]], sig=(k == KD - 1))
                P.op(dve, lambda e: e.tensor_tensor(mk(dtt[:, 0:1], [[32, CH], [1, 32]]),
                                                    mk(Q[3][0:64, 0:1], [[32, CH], [1, 32]]),
                                                    mk(dtb[:, 0:1], [[0, CH], [1, 32]]), ALU.add),
                     [r_Q[3][0], r_par], [r_dt])
                P.op(act, lambda e: e.activation(out=dte[:], in_=dtt[:], func=AF.Exp), [r_dt], [r_dt])
                P.op(act, lambda e: e.activation(out=dtt[:], in_=dte[:], func=AF.Ln, bias=eps_c[0:64, 1:2], scale=1.0),
                     [r_dt, r_const], [r_dt])
                P.op(dve, lambda e: e.tensor_tensor(mk(dtA[:, 0:1], [[32, CH], [1, 32]]),
                                                    mk(dtt[:, 0:1], [[32, CH], [1, 32]]),
                                                    mk(Abc[:, 0:1], [[0, CH], [1, 32]]), ALU.mult),
                     [r_dt, r_par], [r_dt])
                P.op(pe, lambda e: e.matmul(Q[3][0:64, 128:128 + N_], cst[0:64, 256:320], dtA[:], start=True, stop=True),
                     [r_dt, r_const], [r_Q[3][0]], sig=False)
                P.op(pe, lambda e: e.matmul(Q[3][0:64, 256:256 + N_], cst[0:64, 320:384], dtA[:], start=True, stop=True),
                     [r_dt, r_const], [r_Q[3][0]], sig=False)
                P.op(pe, lambda e: e.matmul(Q[3][:, 384:384 + N_], cst[0:64, 832:960], dtA[:], start=True, stop=True),
                     [r_dt, r_const], [r_Q[3][0]])
                P.op(act, lambda e: e.activation(out=acs[:], in_=Q[3][0:64, 128:128 + N_], func=AF.Copy),
                     [r_Q[3][0]], [r_dt])
                P.op(act, lambda e: e.activation(out=dec[:], in_=Q[3][0:64, 256:256 + N_], func=AF.Exp),
                     [r_Q[3][0]], [r_dt])
                P.op(act, lambda e: e.activation(out=cdec[:], in_=Q[3][:, 384:384 + N_], func=AF.Exp),
                     [r_Q[3][0]], [r_dt])
                P.op(dve, lambda e: e.tensor_tensor(dtd[:], dtt[:], dec[:], ALU.mult), [r_dt], [r_dt])
                wq_ = [pre_in[0] if pre_in[0] is not None else load_in(0)]
                pre_in[0] = None
                for g in range(12):
                    wi = wq_.pop(0)
                    if g + 1 < 12:
                        wq_.append(load_in(g + 1))
                    for jp in range(2):
                        pair = []
                        for jj in (2 * jp, 2 * jp + 1):
                            j = g * 4 + jj
                            pi = pcnt[0] % 2
                            pcnt[0] += 1
                            for k in range(KD):
                                P.op(pe, lambda e: e.matmul(pg[pi][:, 0:TM], winp[wi][:, k, jj * 128:(jj + 1) * 128],
                                                            xn[:, k, :], start=(k == 0), stop=(k == KD - 1)),
                                     [r_winp[wi], ph.r_xn], [r_pg[pi]], sig=(k == KD - 1))
                            pair.append((j, pi))
                        if g < 4:
                            for j, pi in pair:
                                P.op(act, lambda e: e.activation(out=zs[:, j, :], in_=pg[pi][:, 0:TM], func=AF.Silu),
                                     [r_pg[pi]], [r_zs])
                            continue
                        ch = [(j - 16, pi, (j - 16) % 2) for j, pi in pair]
                        for jc, pi, ri in ch:
                            P.op(act, lambda e: e.activation(out=raw[ri][:, 3:3 + TM], in_=pg[pi][:, 0:TM], func=AF.Copy),
                                 [r_pg[pi]], [r_raw[ri]])
                            P.op(pool, lambda e: e.tensor_copy(raw[ri][:, 0:3], halo[:, jc, :]), [r_halo], [r_raw[ri]])
                        for jc, pi, ri in ch:
                            P.op(act, lambda e: e.activation(out=acc[ri][:], in_=pg[pi][:, 0:TM], func=AF.Identity,
                                                             scale=convw[:, jc, 3:4], bias=convb[:, jc:jc + 1]),
                                 [r_pg[pi], r_par], [r_acc[ri]])
                        for t in (2, 1, 0):
                            for jc, pi, ri in ch:
                                P.op(dve, lambda e: e.scalar_tensor_tensor(out=acc[ri][:], in0=raw[ri][:, t:t + TM],
                                                                           scalar=convw[:, jc, t:t + 1], in1=acc[ri][:],
                                                                           op0=ALU.mult, op1=ALU.add),
                                     [r_raw[ri], r_par, r_acc[ri]], [r_acc[ri]])
                        for jc, pi, ri in ch:
                            P.op(pool, lambda e: e.tensor_copy(halo[:, jc, :], raw[ri][:, TM:TM + 3]), [r_raw[ri]], [r_halo])
                            if jc < 16:
                                dst_, rd = cvx[:, jc, :], r_cvx
                            elif jc < 24:
                                dst_, rd = BT[:, jc - 16, :], r_BT
                            else:
                                dst_, rd = CT[:, jc - 24, :], r_CT
                            P.op(act, lambda e: e.activation(out=dst_, in_=acc[ri][:], func=AF.Silu), [r_acc[ri]], [rd])
                tp = ti % 2
                P.op(pe, lambda e: e.transpose(Q[3][:, 0:64], acs[0:64, 0:N_], cst[0:64, 0:64]),
                     [r_dt, r_const], [r_Q[3][0]])
                P.op(act, lambda e: e.activation(out=acsT[0:N_, :], in_=Q[3][0:N_, 0:64], func=AF.Copy),
                     [r_Q[3][0]], [r_acsT])
                P.dma(sp, acsd[tp, 0:N_, :], acsT[0:N_, :], [r_acsT], [r_acsd[tp]])

                def load_arb(c):
                    b = c % 2
                    src = acsd[tp, c * 32:(c + 1) * 32, :]
                    P.dma(sp, AcsRowB[b][:].rearrange("p h l -> p (h l)"),
                          bass.AP(src.tensor, src.offset, [[0, 128], [1, 2048]]), [r_acsd[tp]], [r_ARB[b]])

                def front(c):
                    b = c % 2
                    cs = slice(c * 64, (c + 1) * 64)
                    a_c = acs[:, c * 32:c * 32 + 1]
                    if c + 1 < CH:
                        load_arb(c + 1)
                    for g in range(8):
                        P.op(pe, lambda e: e.matmul(Q[2][0:64, g * 128:(g + 1) * 128], BT[:, g, cs], cstb[:, 0:128],
                                                    start=True, stop=True),
                             [r_BT, r_const], [r_Q[2][0], r_Q[2][1]], sig=(g == 7))
                    P.op(act, lambda e: e.activation(out=Btok[b][:].rearrange("p g n -> p (g n)"), in_=Q[2][0:64, :],
                                                     func=AF.Copy), [r_Q[2][0], r_Q[2][1]], [r_Btok[b]])
                    for g in range(8):
                        P.op(pe, lambda e: e.matmul(Q[3][0:64, 512 + g * 64:512 + (g + 1) * 64], BT[:, g, cs],
                                                    CT[:, g, cs], start=True, stop=True),
                             [r_BT, r_CT], [r_Q[3][1]], sig=(g == 7))
                    P.op(act, lambda e: e.activation(out=cbs[:].rearrange("p g l -> p (g l)"), in_=Q[3][0:64, 512:1024],
                                                     func=AF.Copy), [r_Q[3][1]], [r_cbs])
                    for hf in range(2):
                        hs = slice(hf * 16, (hf + 1) * 16)
                        P.op(pool, lambda e: e.tensor_tensor(dseg[:, hs, :], AcsRowB[b][0:64, hs, :],
                                                             mk(acs[:, c * 32 + hf * 16:c * 32 + hf * 16 + 1], [[1, 16], [0, 64]]),
                                                             ALU.subtract), [r_ARB[b], r_dt], [r_dsegh[hf]])
                    for hf in range(2):
                        hs = slice(hf * 16, (hf + 1) * 16)
                        P.op(dve, lambda e: e.tensor_tensor(dseg[:, hs, :], dseg[:, hs, :],
                                                            mk(cst[0:64, 384:385], [[0, 16], [1, 64]]), ALU.min),
                             [r_dsegh[hf], r_const], [r_dsegh[hf]])
                        P.op(act, lambda e: e.activation(out=dseg[:, hs, :], in_=dseg[:, hs, :], func=AF.Exp),
                             [r_dsegh[hf]], [r_dsegh[hf]])
                    P.op(dve, lambda e: e.tensor_tensor(
                        mk(WT[:, 0, 0:1], [[256, 4], [64, 4], [1, 64]]),
                        mk(dseg[:, 0, 0:1], [[256, 4], [64, 4], [1, 64]]),
                        mk(cbs[:, 0, 0:1], [[64, 4], [0, 4], [1, 64]]), ALU.mult),
                        [r_dsegh[0], r_cbs], [r_WT])
                    for g in range(4, 8):
                        P.op(pool, lambda e: e.tensor_tensor(WT[:, 4 * g:4 * g + 4, :], dseg[:, 4 * g:4 * g + 4, :],
                                                             mk(cbs[:, g, 0:1], [[0, 4], [1, 64]]), ALU.mult),
                             [r_dsegh[1], r_cbs], [r_WT2])
                    for hf in range(2):
                        for i in range(8):
                            P.op(pe, lambda e: e.transpose(Q[0][0:64, i * 128:(i + 1) * 128], cvx[:, hf * 8 + i, cs],
                                                           ident), [r_cvx, r_const], [r_Q[0][0], r_Q[0][1]],
                                 sig=(i == 7))
                        hb = c * 32 + hf * 16
                        P.op(dve, lambda e: e.tensor_tensor(padded(Xp, hf), mk(Q[0][0:64, 0:1], [[128, 8], [64, 2], [1, 64]]),
                                                            mk(dtt[:, hb:hb + 1], [[2, 8], [1, 2], [0, 64]]), ALU.mult),
                             [r_Q[0][0], r_Q[0][1], r_dt], [r_Xp])
                        P.op(dve, lambda e: e.tensor_tensor(Xd[b][:, hf * 16:(hf + 1) * 16, :],
                                                            mk(Q[0][0:64, 0:1], [[64, 16], [1, 64]]),
                                                            mk(dtd[:, hb:hb + 1], [[1, 16], [0, 64]]), ALU.mult),
                             [r_Q[0][0], r_Q[0][1], r_dt], [r_Xd[b]])
                    for i in range(16):
                        for hh in range(2):
                            P.op(pe, lambda e: e.matmul(Q[2][:, i * 64:(i + 1) * 64], Xp[:, 2 * i + hh, :],
                                                        WT[:, 2 * i + hh, :], start=(hh == 0), stop=(hh == 1)),
                                 [r_Xp, r_WT, r_WT2], [r_Q[2][i // 8]], sig=(hh == 1 and i % 8 == 7))
                    P.op(act, lambda e: e.activation(out=ydsb[b][:], in_=Q[2][:, :], func=AF.Copy),
                         [r_Q[2][0], r_Q[2][1]], [r_ydsb[b]])
                    for hh in range(2):
                        ps_ = slice(hh * 64, (hh + 1) * 64)
                        P.op(act, lambda e: e.activation(out=eA2[b][ps_, :].rearrange("p (i l) -> p i l", l=64),
                                                         in_=mk(AcsRowB[b][ps_, hh, 0:1], [[128, 16], [1, 64]]),
                                                         func=AF.Exp), [r_ARB[b]], [r_eA2[b]])

                def back(c):
                    b = c % 2
                    cs = slice(c * 64, (c + 1) * 64)
                    for i in range(16):
                        for hh in range(2):
                            P.op(pe, lambda e: e.matmul(Q[1][:, i * 64:(i + 1) * 64], stbp[:, 2 * i + hh, :],
                                                        CT[:, i // 2, cs], start=(hh == 0), stop=(hh == 1)),
                                 [r_stbp, r_CT], [r_Q[1][i // 8]], sig=(hh == 1 and i % 8 == 7))
                    P.op(pool, lambda e: e.tensor_tensor(st[:], st[:], mk(cdec[:, c * 32:c * 32 + 1], [[1, 32], [0, 64]]),
                                                         ALU.mult), [r_st, r_dt], [r_st])
                    P.op(dve, lambda e: e.tensor_tensor(tY[:], Q[1][:, :], eA2[b][:], ALU.mult),
                         [r_Q[1][0], r_Q[1][1], r_eA2[b]], [r_tY])
                    P.op(pool, lambda e: e.tensor_tensor(ybuf[:, :, cs], mk(tY[:, 0:1], [[64, 16], [1, 64]]),
                                                         mk(ydsb[b][:, 0:1], [[64, 16], [1, 64]]), ALU.add),
                         [r_tY, r_ydsb[b]], [r_ybuf])
                    for hf in range(2):
                        for gi in range(4):
                            g = hf * 4 + gi
                            P.op(pe, lambda e: e.matmul(Q[1][:, gi * 256:(gi + 1) * 256], Btok[b][:, g, :],
                                                        Xd[b][:, g * 4:(g + 1) * 4, :], start=True, stop=True),
                                 [r_Btok[b], r_Xd[b]], [r_Q[1][gi // 2]], sig=(gi % 2 == 1))
                        sth = st[:, hf * 16:(hf + 1) * 16, :]
                        P.op(dve, lambda e: e.tensor_tensor(sth, sth, mk(Q[1][:, 0:1], [[64, 16], [1, 64]]), ALU.add),
                             [r_st, r_Q[1][0], r_Q[1][1]], [r_st])
                        P.op(act, lambda e: e.activation(out=padded(stbp, hf),
                                                         in_=mk(st[:, hf * 16, 0:1], [[128, 8], [64, 2], [1, 64]]),
                                                         func=AF.Copy), [r_st], [r_stbp])

                load_arb(0)
                front(0)
                for c in range(CH):
                    if c + 1 < CH:
                        front(c + 1)
                    back(c)
                for k in range(16):
                    P.op(dve, lambda e: e.scalar_tensor_tensor(out=cvx[:, k, :], in0=cvx[:, k, :], scalar=Dcol[:, k:k + 1],
                                                               in1=ybuf[:, k, :], op0=ALU.mult, op1=ALU.add),
                         [r_cvx, r_ybuf, r_par], [r_cvx])
                P.op(dve, lambda e: e.tensor_tensor(cvx[:], cvx[:], zs[:], ALU.mult), [r_cvx, r_zs], [r_cvx])
                P.op(act, lambda e: e.activation(out=sq2[:], in_=cvx[:], func=AF.Square), [r_cvx], [r_sq2])
                for half in range(2):
                    for g4 in range(4):
                        g = half * 4 + g4
                        for kk in range(2):
                            P.op(pe, lambda e: e.matmul(Q[3][:, g4 * 256:g4 * 256 + TM], ones_b, sq2[:, 2 * g + kk, :],
                                                        start=(kk == 0), stop=(kk == 1)),
                                 [r_const, r_sq2], [r_Q[3][g4 // 2]], sig=(kk == 1 and g4 % 2 == 1))
                    P.op(act, lambda e: e.activation(out=rg[:], in_=Q[3][:, :], func=AF.Ln,
                                                     bias=eps_c[:, 0:1], scale=1.0 / 256),
                         [r_Q[3][0], r_Q[3][1], r_const], [r_rg])
                    P.op(act, lambda e: e.activation(out=rg[:], in_=rg[:], func=AF.Exp, scale=-0.5), [r_rg], [r_rg])
                    for g4 in range(4):
                        for kk in range(2):
                            k = 2 * (half * 4 + g4) + kk
                            P.op(dve, lambda e: e.scalar_tensor_tensor(out=ynb[:, k, :], in0=cvx[:, k, :],
                                                                       scalar=onorm[:, k:k + 1],
                                                                       in1=rg[:, g4 * 256:g4 * 256 + TM],
                                                                       op0=ALU.mult, op1=ALU.mult),
                                 [r_cvx, r_par, r_rg], [r_ynb])
                nxt = load_o(0)
                for o in range(4):
                    wi = nxt
                    if o + 1 < 4:
                        nxt = load_o(o + 1)
                    elif ti + 1 < S // TM:
                        pre_in[0] = load_in(0)
                    oi = o % 2
                    for mm in range(2):
                        m = o * 2 + mm
                        pi = pcnt[0] % 2
                        pcnt[0] += 1
                        for kk in range(16):
                            P.op(pe, lambda e: e.matmul(py[pi][:, 0:TM], wo[wi][:, kk, mm * 128:(mm + 1) * 128],
                                                        ynb[:, kk, :], start=(kk == 0), stop=(kk == 15)),
                                 [r_wo[wi], r_ynb], [r_py[pi]], sig=(kk == 15))
                        P.op(dve, lambda e: e.tensor_tensor(xo[oi][:, mm, :], py[pi][:, 0:TM], xt[:, m, :], ALU.add),
                             [r_py[pi], ph.r_xT[0]], [r_xo[oi]])
                    P.dma(act, xres_tile(t0, TM, slice(o * 256, (o + 1) * 256)), xo[oi][:], [r_xo[oi]], [rx])
            end_phase(ph)

        def attention(j):
            l = NA + j
            ph = new_phase(T, nx=1)
            xt, xn = ph.xT[0], ph.xn
            NQ = T // 128
            sbp = lambda n, sh, dt: P.sb(n, sh, dt, ph.es)
            R = lambda n, dma=False: P.res(n, dma=dma)
            xnkv = sbp("xnkv", [128, KD, T], BF16)
            NWB = 3
            wbuf = [sbp("wbuf", [128, KD, 512], BF16) for i in range(NWB)]
            r_wbuf = [R("wbuf%d" % i, True) for i in range(NWB)]
            QTz = sbp("QTz", [128, 8, NQ, 2, 128], BF16)
            kT = sbp("kT", [128, 8, T], BF16)
            vt = sbp("vt", [128, NQ, 1024], BF16)
            NR = 6
            KTr = [sbp("KTr", [128, 8, 128], BF16) for i in range(NR)]
            Vr = [sbp("Vr", [128, 1024], BF16) for i in range(NR)]
            r_KTr = [R("KTr%d" % i, True) for i in range(NR)]
            r_Vr = [R("Vr%d" % i, True) for i in range(NR)]
            PT = [sbp("PT", [128, 5, 2, 128], BF16) for i in range(2)]
            r_PT = [R("PT0"), R("PT1")]
            Hk = sbp("Hk", [128, 16, 256], F32)
            Hkb = sbp("Hkb", [128, 16, 256], BF16)
            b0col = sbp("b0col", [128, 16], F32)
            gq = sbp("gq", [128, 2], F32)
            aT = sbp("aT", [128, 8, T], BF16)
            ksq = sbp("ksq", [128, 512], BF16)
            rk = sbp("rk", [128, 512], F32)
            rden = [sbp("rden", [128, 256], F32) for i in range(2)]
            xo = [sbp("xo", [128, 4, T], F32) for i in range(2)]
            r_xo = [R("xo0"), R("xo1")]
            r_par = R("par", True)
            r_vext = R("vext", True)
            r_kvd = R("kvd", True)
            r_xnkv, r_QTz, r_kT, r_vt, r_aT, r_ksq, r_rk, r_rden = [R(n) for n in (
                "xnkv", "QTz", "kT", "vt", "aT", "ksq", "rk", "rden0")]
            r_rden = [r_rden, R("rden1")]
            r_rden2 = [R("rden2a"), R("rden2b")]

            for hh in range(2):
                src = io["att_q_norm"][j, :]
                P.dma(sp, gq[hh * 64:(hh + 1) * 64, 0:1], bass.AP(src.tensor, src.offset, [[1, 64], [1, 1]]), [], [r_par])
                src = io["k_norm"][:]
                P.dma(sp, gq[hh * 64:(hh + 1) * 64, 1:2], bass.AP(src.tensor, src.offset, [[1, 64], [1, 1]]), [], [r_par])
                src = io["att_rel_bias"][j, :, 0:1]
                P.dma(sp, b0col[hh * 64:(hh + 1) * 64, :], bass.AP(src.tensor, src.offset, [[0, 64], [257, 16]]), [],
                      [r_par], allow_slow_non_contiguous=True)
            P.op(dve, lambda e: e.tensor_scalar(gq[:, 0:1], gq[:, 0:1], 0.125, None, ALU.mult), [r_par], [r_par])
            rb = io["att_rel_bias"][j]
            vtmp = sbp("vtmp", [16, 384], F32)
            r_vtmp = R("vtmp", True)
            P.dma(sp, vtmp[:, 127:383], rb[:, 0:256], [], [r_vtmp])
            P.op(pool, lambda e: e.memset(vtmp[:, 0:127], 0.0), [], [r_vtmp])
            P.op(dve, lambda e: e.tensor_scalar(vtmp[:, 128:383], vtmp[:, 128:383], vtmp[:, 127:128], None,
                                                ALU.subtract), [r_vtmp], [r_vtmp])
            P.op(dve, lambda e: e.memset(vtmp[:, 127:128], 0.0), [r_vtmp], [r_vtmp])
            P.dma(sp, vext[:, 0:383], vtmp[:, 0:383], [r_vtmp], [r_vext])
            for h8 in range(2):
                src = vext[h8 * 8:(h8 + 1) * 8, :]
                P.dma(sp, Hk[:, h8 * 8:(h8 + 1) * 8, :], bass.AP(src.tensor, src.offset, [[1, 128], [384, 8], [1, 256]]),
                      [r_vext], [r_par])
            P.op(dve, lambda e: e.tensor_copy(Hkb[:], Hk[:]), [r_par], [r_par])
            P.op(pool, lambda e: e.memset(QTz[:], 0.0), [], [r_QTz])
            for i in range(2):
                P.op(pool, lambda e: e.memset(PT[i][:], 0.0), [], [r_PT[i]])

            wctr = [0]
            pcnt = [0]

            wseq = []
            for ti_ in range(NT):
                if j == 0:
                    wseq += [(wb["w_kv"], 0), (wb["w_kv"], 512), (wb["w_kv"], 1024), (wb["w_kv"], 1536)]
                wseq += [(wb["att_w_q"][j], 0), (wb["att_w_q"][j], 512), (wb["att_w_o"][j], 0), (wb["att_w_o"][j], 512)]
            wissued = [0]

            def issue_w():
                if wissued[0] < len(wseq):
                    W, c0 = wseq[wissued[0]]
                    i = wissued[0] % NWB
                    P.dma(sp, wbuf[i][:], W[:, c0:c0 + 512].rearrange("(k p) c -> p k c", p=128), [r_wconv[l]], [r_wbuf[i]])
                    wissued[0] += 1

            def load_w(W, c0):
                i = wctr[0] % NWB
                assert wseq[wctr[0]][1] == c0
                wctr[0] += 1
                while wissued[0] < min(wctr[0] + NWB - 1, len(wseq)):
                    issue_w()
                return i

            issue_w()
            issue_w()

            ksq4 = [ksq] + [sbp("ksq", [128, 512], BF16) for i in range(3)]
            rk4 = [rk] + [sbp("rk", [128, 512], F32) for i in range(3)]
            r_ksq4 = [r_ksq] + [R("ksq%d" % i) for i in range(1, 4)]
            r_rk4 = [r_rk] + [R("rk%d" % i) for i in range(1, 4)]

            def proj_heads(W, c0base, src, r_src, gcol, dest):
                for cg in range(2):
                    wi = load_w(W, c0base + cg * 512)
                    mains = [PS[:, i * 512:(i + 1) * 512] for i in range(4)]
                    stats = [PS[:, (4 + i) * 512:(5 + i) * 512] for i in range(4)]
                    for mm in range(4):
                        for k in range(KD):
                            P.op(pe, lambda e: e.matmul(mains[mm], wbuf[wi][:, k, mm * 128:(mm + 1) * 128], src[:, k, :],
                                                        start=(k == 0), stop=(k == KD - 1)),
                                 [r_wbuf[wi], r_src], [r_bank[mm]], sig=(k == KD - 1))
                    for mm in range(4):
                        P.op(act, lambda e: e.activation(out=ksq4[mm][:], in_=mains[mm], func=AF.Square),
                             [r_bank[mm]], [r_ksq4[mm]])
                    for mm in range(4):
                        P.op(pe, lambda e: e.matmul(stats[mm], cstb[:, 704:832], ksq4[mm][:], start=True, stop=True),
                             [r_ksq4[mm], r_const], [r_bank[4 + mm]])
                    for mm in range(4):
                        P.op(act, lambda e: e.activation(out=rk4[mm][:], in_=stats[mm], func=AF.Ln, bias=eps_c[:, 0:1],
                                                         scale=1.0 / 64), [r_bank[4 + mm], r_const], [r_rk4[mm]])
                    for mm in range(4):
                        P.op(act, lambda e: e.activation(out=rk4[mm][:], in_=rk4[mm][:], func=AF.Exp, scale=-0.5),
                             [r_rk4[mm]], [r_rk4[mm]])
                    for mm in range(4):
                        m = cg * 4 + mm
                        for (ps_, dst_, rd) in dest(m):
                            shp = (lambda a_: a_.rearrange("p (q c) -> p q c", c=128)) if len(dst_.shape) == 3 else (lambda a_: a_)
                            P.op(dve, lambda e: e.scalar_tensor_tensor(out=dst_, in0=shp(mains[mm][ps_, :]), scalar=gcol[ps_, :],
                                                                       in1=shp(rk4[mm][ps_, :]), op0=ALU.mult, op1=ALU.mult),
                                 [r_bank[mm], r_rk4[mm], r_par], [rd])

            for ti in range(NT):
                t0 = ti * T
                P.dma(sp, xt[:], xres_tile(t0, T), [r_xres[ti]], [ph.r_xT[0]])
                rms_stats(ph, 0)
                rms_apply(ph, 0, G_ATT + j, xn, ph.r_xn)
                if j == 0:
                    rms_apply(ph, 0, G_KV, xnkv, r_xnkv)
                    proj_heads(wb["w_kv"], 0, xnkv, r_xnkv, gq[:, 1:2],
                               lambda m: [(slice(0, 128), kT[:, m, :], r_kT)])
                    P.dma(sp, KTd[:, t0:t0 + T].rearrange("(m p) t -> p m t", p=128), kT[:], [r_kT], [r_kvd])
                    for cg in range(2):
                        wi = load_w(wb["w_kv"], 1024 + cg * 512)
                        for tb in range(NQ):
                            pi = pcnt[0] % 2
                            pcnt[0] += 1
                            for k in range(KD):
                                P.op(pe, lambda e: e.matmul(pu[pi], xnkv[:, k, tb * 128:(tb + 1) * 128], wbuf[wi][:, k, :],
                                                            start=(k == 0), stop=(k == KD - 1)),
                                     [r_wbuf[wi], r_xnkv], [r_pu[pi]], sig=(k == KD - 1))
                            P.op(act, lambda e: e.activation(out=vt[:, tb, cg * 512:(cg + 1) * 512], in_=pu[pi],
                                                             func=AF.Copy), [r_pu[pi]], [r_vt])
                    P.dma(sp, Vd[t0:t0 + T, :].rearrange("(tb p) c -> p tb c", p=128), vt[:], [r_vt], [r_kvd])
                proj_heads(wb["att_w_q"][j], 0, xn, ph.r_xn, gq[:, 0:1],
                           lambda m: [(slice(0, 64), QTz[0:64, m, :, 0, :], r_QTz),
                                      (slice(64, 128), QTz[64:128, m, :, 1, :], r_QTz)])
                def load_kv(KB):
                    sl = KB % NR
                    P.dma(sp, KTr[sl][:], KTd[:, KB * 128:(KB + 1) * 128].rearrange("(m p) t -> p m t", p=128),
                          [r_kvd], [r_KTr[sl]])
                    P.dma(sp, Vr[sl][:], Vd[KB * 128:(KB + 1) * 128, :], [r_kvd], [r_Vr[sl]])

                def s1(QB, qb, m, par, vb):
                    base = par * 1536
                    rs = [r_bank[par * 3 + i] for i in range(3)]
                    for jb in vb:
                        slot = (QB - 4 + jb) % NR
                        o_ = PS[:, base + jb * 256:base + (jb + 1) * 256]
                        last = (jb == vb[-1])
                        if jb < 3:
                            P.op(pe, lambda e: e.matmul(o_, KTr[slot][:, m, :], QTz[:, m, qb, :, :], start=True, stop=True),
                                 [r_KTr[slot], r_QTz], rs, sig=last)
                        else:
                            for hh in range(2):
                                oh = o_[:, hh * 128:(hh + 1) * 128]
                                P.op(pe, lambda e: e.matmul(oh, KTr[slot][:, m, :], QTz[:, m, qb, hh, :], start=True,
                                                            stop=False), [r_KTr[slot], r_QTz], rs, sig=False)
                                P.op(pe, lambda e: e.matmul(oh, Hkb[:, 2 * m + hh, (jb - 3) * 128:(jb - 2) * 128],
                                                            cstb[:, 128:256], start=False, stop=True),
                                     [r_par, r_const], rs, sig=(last and hh == 1))

                def s2(par, vb):
                    base = par * 1536
                    rs = [r_bank[par * 3 + i] for i in range(3)]
                    pt = PT[par]
                    sc = lambda p0, p1, jb, c0: mk(PS[p0:p1, base + jb * 256 + c0:base + jb * 256 + c0 + 1], [[128, 2], [1, 64]])
                    ex = lambda o_, i_: P.op(act, lambda e: e.activation(out=o_, in_=i_, func=AF.Exp), rs, [r_PT[par]])
                    if 0 in vb:
                        ex(pt[:, 0, :, 0:64], sc(0, 128, 0, 0))
                        ex(pt[64:128, 0, :, 64:128], sc(64, 128, 0, 64))
                    mid = [jb for jb in (1, 2, 3) if jb in vb]
                    if mid:
                        a, b = mid[0], mid[-1] + 1
                        ex(pt[:, a:b, :, :].rearrange("p j h q -> p (j h q)"), PS[:, base + a * 256:base + b * 256])
                    ex(pt[0:64, 4, :, 0:64], sc(0, 64, 4, 0))
                    ex(pt[:, 4, :, 64:128], sc(0, 128, 4, 64))

                def s3(QB, qb, m, par, vb):
                    ob = 6 + par
                    o_ = PS[:, ob * 512:ob * 512 + 256]
                    d_ = PS[:, ob * 512 + 256:ob * 512 + 512]
                    for i, jb in enumerate(vb):
                        slot = (QB - 4 + jb) % NR
                        P.op(pe, lambda e: e.matmul(o_, Vr[slot][:, m * 128:(m + 1) * 128], PT[par][:, jb, :, :],
                                                    start=(i == 0), stop=(i == len(vb) - 1)),
                             [r_Vr[slot], r_PT[par]], [r_bank[ob]], sig=False)
                    for i, jb in enumerate(vb):
                        P.op(pe, lambda e: e.matmul(d_, cstb[:, 832:960], PT[par][:, jb, :, :],
                                                    start=(i == 0), stop=(i == len(vb) - 1)),
                             [r_const, r_PT[par]], [r_bank[ob]], sig=(i == len(vb) - 1))
                    P.op(dve, lambda e: e.reciprocal(rden[par][:], d_), [r_bank[ob]], [r_rden[par]])
                    qc = slice(qb * 128, (qb + 1) * 128)
                    for hh in range(2):
                        ps_ = slice(hh * 64, (hh + 1) * 64)
                        P.op(dve, lambda e: e.tensor_tensor(aT[ps_, m, qc], o_[ps_, hh * 128:(hh + 1) * 128],
                                                            rden[par][ps_, hh * 128:(hh + 1) * 128], ALU.mult),
                             [r_bank[ob], r_rden[par]], [r_aT])

                seq = []
                for qb in range(NQ):
                    QB = ti * NQ + qb
                    vb = [jb for jb in range(5) if QB - 4 + jb >= 0]
                    for m in range(8):
                        seq.append((QB, qb, m, vb))
                if ti == 0 or j == 0:
                    load_kv(ti * NQ)
                for idx, (QB, qb, m, vb) in enumerate(seq):
                    par = idx % 2
                    if m == 0:
                        nb = QB + 1
                        if nb < S // 128 and (j == 1 or qb + 1 < NQ):
                            load_kv(nb)
                    if idx == 0:
                        s1(QB, qb, m, par, vb)
                    s2(par, vb)
                    if idx + 1 < len(seq):
                        QB2, qb2, m2, vb2 = seq[idx + 1]
                        s1(QB2, qb2, m2, 1 - par, vb2)
                    s3(QB, qb, m, par, vb)
                for o2 in range(2):
                    wi = load_w(wb["att_w_o"][j], o2 * 512)
                    oi = o2 % 2
                    for mm in range(4):
                        m = o2 * 4 + mm
                        pi = pcnt[0] % 2
                        pcnt[0] += 1
                        for k in range(KD):
                            P.op(pe, lambda e: e.matmul(py[pi], wbuf[wi][:, k, mm * 128:(mm + 1) * 128], aT[:, k, :],
                                                        start=(k == 0), stop=(k == KD - 1)),
                                 [r_wbuf[wi], r_aT], [r_py[pi]], sig=(k == KD - 1))
                        P.op(dve, lambda e: e.tensor_tensor(xo[oi][:, mm, :], py[pi], xt[:, m, :], ALU.add),
                             [r_py[pi], ph.r_xT[0]], [r_xo[oi]])
                    P.dma(act, xres_tile(t0, T, slice(o2 * 512, (o2 + 1) * 512)), xo[oi][:], [r_xo[oi]], [r_xres[ti]])
            end_phase(ph)

        prologue()
        for l in range(L):
            if "noffn" not in cfg.mode:
                ffn(l, 0)
            if l < NA:
                if "nomix" not in cfg.mode:
                    mamba(l)
            else:
                if "nomix" not in cfg.mode:
                    attention(l - NA)
            if "noffn" not in cfg.mode:
                ffn(l, 1)
        epilogue()
        P.finish(sp)
    return nc


_CACHE = {}


def kernel(**inputs):
    S = inputs["x"].shape[1]
    cfg = Cfg(S)
    key = (S,)
    if key not in _CACHE:
        _CACHE[key] = build(cfg)
    nc = _CACHE[key]
    consts = make_consts()
    shared = {k: np.ascontiguousarray(np.asarray(v, dtype=np.float32)) for k, v in inputs.items() if k != "x"}
    xin = np.asarray(inputs["x"], dtype=np.float32)
    in_maps = []
    for c in range(NCORES):
        m = {"x": np.ascontiguousarray(xin[c])}
        m.update(shared)
        m.update(consts)
        in_maps.append(m)
    res = run_bass_kernel_spmd(nc, in_maps, core_ids=list(range(NCORES)))
    return np.stack([np.asarray(r["out"]) for r in res.results], axis=0).astype(np.float32)
```
